# Optimizing a Trainium2 kernel written in Bass

```python
import math
import jax, jax.numpy as jnp
from jax import lax
import numpy as np

D_MODEL = 1024
BATCH = 8
SEQ = 2048
DEPTH = 1

DN_HEADS = 8
DN_HEAD_DIM = 128
DN_W = DN_HEADS * DN_HEAD_DIM
DN_CONV = 4
DN_CHUNK = 64
NSA_HEADS = 16
NSA_GROUPS = 4
NSA_REP = NSA_HEADS // NSA_GROUPS
NSA_HEAD_DIM = 64
NSA_W = NSA_HEADS * NSA_HEAD_DIM
NSA_KV_W = NSA_GROUPS * NSA_HEAD_DIM
CMP_LEN = 32
CMP_STRIDE = 16
CMP_HIDDEN = 2 * NSA_HEAD_DIM
SLC_LEN = 64
SLC_TOPK = 8
WINDOW = 256
ATT_QBLK = 128
SLC_QBLK = 64
D_FF = 4 * D_MODEL
IN_WIDTH = 4 * DN_W + 2 * DN_HEADS + NSA_W + 6 * NSA_KV_W + 3 * NSA_HEADS + 2 * D_MODEL
EPS = 1e-6
NEG = -1e30
FORCE = 1e9

kernel_name = "hybrid_gdn_nsa_parallel_block"


def _in_split_points():
    sizes = [3 * DN_W, DN_W, DN_HEADS, DN_HEADS,
             NSA_W, NSA_KV_W, NSA_KV_W, NSA_KV_W, NSA_KV_W, NSA_KV_W, NSA_KV_W, 3 * NSA_HEADS,
             D_MODEL, D_MODEL]
    return [int(v) for v in np.cumsum(sizes)[:-1]]


def rmsnorm(x, gain):
    xf = x.astype(jnp.float32)
    y = xf * lax.rsqrt(jnp.mean(xf * xf, axis=-1, keepdims=True) + EPS)
    return (y * gain.astype(jnp.float32)).astype(x.dtype)


def l2norm(x):
    xf = x.astype(jnp.float32)
    return (xf * lax.rsqrt(jnp.sum(xf * xf, axis=-1, keepdims=True) + EPS)).astype(x.dtype)


def alibi_slopes(n_heads):
    return jnp.asarray(2.0 ** (-8.0 * np.arange(1, n_heads + 1) / n_heads), jnp.float32)


def causal_depthwise_conv(x, w):
    k, c = w.shape
    return lax.conv_general_dilated(x, w[:, None, :].astype(x.dtype), window_strides=(1,),
                                    padding=[(k - 1, 0)], dimension_numbers=('NWC', 'WIO', 'NWC'),
                                    feature_group_count=c)


def chunk_gated_delta_rule(q, k, v, g, beta):
    B, S, H, dk = q.shape
    dv = v.shape[-1]
    C = DN_CHUNK
    N = S // C
    f32 = jnp.float32

    def chunks(t):
        t = t.astype(f32).reshape((B, N, C, H) + t.shape[3:])
        return jnp.moveaxis(t, (1, 3), (0, 2))

    q = chunks(q) * (dk ** -0.5)
    k = chunks(k)
    v = chunks(v)
    beta = chunks(beta)
    gc = jnp.cumsum(chunks(g), axis=-1)
    kb = k * beta[..., None]
    vb = v * beta[..., None]
    causal = jnp.tril(jnp.ones((C, C), bool))
    strict = jnp.tril(jnp.ones((C, C), bool), -1)
    decay = jnp.exp(jnp.where(causal, gc[..., :, None] - gc[..., None, :], -jnp.inf))
    a = jnp.where(strict, jnp.einsum('nbhid,nbhjd->nbhij', kb, k) * decay, 0.0)
    t_mat = a + jnp.eye(C, dtype=f32)
    u = lax.linalg.triangular_solve(t_mat, vb, left_side=True, lower=True)
    w = lax.linalg.triangular_solve(t_mat, kb * jnp.exp(gc)[..., None], left_side=True, lower=True)
    qk = jnp.einsum('nbhid,nbhjd->nbhij', q, k) * decay
    q_dec = q * jnp.exp(gc)[..., None]
    k_end = k * jnp.exp(gc[..., -1:] - gc)[..., None]
    g_last = jnp.exp(gc[..., -1])

    def step(state, inp):
        qd, ke, u_c, w_c, qk_c, gl = inp
        v_new = u_c - jnp.einsum('bhcd,bhde->bhce', w_c, state)
        o = jnp.einsum('bhcd,bhde->bhce', qd, state) + jnp.einsum('bhij,bhje->bhie', qk_c, v_new)
        state = state * gl[..., None, None] + jnp.einsum('bhcd,bhce->bhde', ke, v_new)
        return state, o

    s0 = jnp.zeros((B, H, dk, dv), f32)
    _, o = lax.scan(step, s0, (q_dec, k_end, u, w, qk, g_last))
    return jnp.moveaxis(o, (0, 2), (1, 3)).reshape(B, S, H, dv)


def gated_deltanet(qkv, z, b_logit, a_logit, conv_w, a_log, dt_bias, out_norm):
    B, S, _ = qkv.shape
    f32 = jnp.float32
    qkv = jax.nn.silu(causal_depthwise_conv(qkv, conv_w))
    q, k, v = jnp.split(qkv, 3, axis=-1)
    q = l2norm(q.reshape(B, S, DN_HEADS, DN_HEAD_DIM))
    k = l2norm(k.reshape(B, S, DN_HEADS, DN_HEAD_DIM))
    v = v.reshape(B, S, DN_HEADS, DN_HEAD_DIM)
    beta = jax.nn.sigmoid(b_logit.astype(f32))
    g = -jnp.exp(a_log.astype(f32)) * jax.nn.softplus(a_logit.astype(f32) + dt_bias.astype(f32))
    o = chunk_gated_delta_rule(q, k, v, g, beta)
    o = rmsnorm(o, out_norm) * jax.nn.silu(z.reshape(B, S, DN_HEADS, DN_HEAD_DIM).astype(f32))
    return o.reshape(B, S, DN_W).astype(qkv.dtype)


def cmp_to_slc_overlap(n_cmp, n_sel):
    c0 = np.arange(n_cmp)[:, None] * CMP_STRIDE
    j0 = np.arange(n_sel)[None, :] * SLC_LEN
    ov = np.clip(np.minimum(c0 + CMP_LEN, j0 + SLC_LEN) - np.maximum(c0, j0), 0, None) / CMP_LEN
    return jnp.asarray(ov, jnp.float32)


def compress_blocks(t, pos_emb, w1, w2, idx):
    B, _, G, d = t.shape
    n_cmp = idx.shape[0]
    blk = t[:, idx] + pos_emb[:, None, :].astype(t.dtype)
    blk = jnp.moveaxis(blk, 3, 2).reshape(B, n_cmp, G, CMP_LEN * d)
    return jax.nn.gelu(blk @ w1) @ w2


def window_branch(q, k, v, slopes):
    B, S, G, R, _ = q.shape
    span = WINDOW + ATT_QBLK
    kp = jnp.pad(k, ((0, 0), (WINDOW, 0), (0, 0), (0, 0)))
    vp = jnp.pad(v, ((0, 0), (WINDOW, 0), (0, 0), (0, 0)))

    def block(i):
        t0 = i * ATT_QBLK
        qb = lax.dynamic_slice_in_dim(q, t0, ATT_QBLK, axis=1)
        kb = lax.dynamic_slice_in_dim(kp, t0, span, axis=1)
        vb = lax.dynamic_slice_in_dim(vp, t0, span, axis=1)
        qpos = t0 + jnp.arange(ATT_QBLK)
        kpos = t0 - WINDOW + jnp.arange(span)
        dist = qpos[:, None] - kpos[None, :]
        mask = (dist >= 0) & (dist < WINDOW) & (kpos[None, :] >= 0)
        s = jnp.einsum('bqgrd,bkgd->bgrqk', qb, kb).astype(jnp.float32) - slopes[:, :, None, None] * dist
        p = jax.nn.softmax(jnp.where(mask, s, NEG), axis=-1).astype(vb.dtype)
        return jnp.einsum('bgrqk,bkge->bqgre', p, vb)

    o = lax.map(block, jnp.arange(S // ATT_QBLK))
    return jnp.moveaxis(o, 0, 1).reshape(B, S, G, R, -1)


def selection_branch(q, k, v, sel_idx, slopes):
    B, S, G, R, dk = q.shape
    dv = v.shape[-1]
    n_sel = S // SLC_LEN
    K = sel_idx.shape[-1]
    kblk = jnp.moveaxis(k.reshape(B, n_sel, SLC_LEN, G, dk), 3, 1).reshape(B, G, n_sel, SLC_LEN * dk)
    vblk = jnp.moveaxis(v.reshape(B, n_sel, SLC_LEN, G, dv), 3, 1).reshape(B, G, n_sel, SLC_LEN * dv)
    gather = jax.vmap(jax.vmap(lambda tab, ix: tab[ix]))

    def block(i):
        t0 = i * SLC_QBLK
        qb = lax.dynamic_slice_in_dim(q, t0, SLC_QBLK, axis=1)
        ib = lax.dynamic_slice_in_dim(sel_idx, t0, SLC_QBLK, axis=2)
        flat = ib.reshape(B, G, SLC_QBLK * K)
        kg = gather(kblk, flat).reshape(B, G, SLC_QBLK, K * SLC_LEN, dk)
        vg = gather(vblk, flat).reshape(B, G, SLC_QBLK, K * SLC_LEN, dv)
        kpos = (ib[..., None] * SLC_LEN + jnp.arange(SLC_LEN)).reshape(B, G, SLC_QBLK, K * SLC_LEN)
        qpos = t0 + jnp.arange(SLC_QBLK)
        dist = (qpos[:, None] - kpos)[:, :, None]
        s = jnp.einsum('bqgrd,bgqkd->bgrqk', qb, kg).astype(jnp.float32) - slopes[None, :, :, None, None] * dist
        p = jax.nn.softmax(jnp.where(dist >= 0, s, NEG), axis=-1).astype(vg.dtype)
        return jnp.einsum('bgrqk,bgqke->bqgre', p, vg)

    o = lax.map(block, jnp.arange(S // SLC_QBLK))
    return jnp.moveaxis(o, 0, 1).reshape(B, S, G, R, dv)


def nsa_mixer(q, k_c, v_c, k_s, v_s, k_w, v_w, gate_logits, q_norm, kn_cmp, kn_slc, kn_win,
              cmp_pos_k, cmp_w1_k, cmp_w2_k, cmp_pos_v, cmp_w1_v, cmp_w2_v):
    B, S, _ = q.shape
    G, R, d = NSA_GROUPS, NSA_REP, NSA_HEAD_DIM
    f32 = jnp.float32
    slopes = alibi_slopes(NSA_HEADS).reshape(G, R)
    pos = jnp.arange(S)
    q = (rmsnorm(q.reshape(B, S, NSA_HEADS, d), q_norm) * (d ** -0.5)).reshape(B, S, G, R, d)
    kv = lambda t: t.reshape(B, S, G, d)

    n_cmp = (S - CMP_LEN) // CMP_STRIDE + 1
    idx = np.arange(n_cmp)[:, None] * CMP_STRIDE + np.arange(CMP_LEN)[None, :]
    kc = rmsnorm(compress_blocks(kv(k_c), cmp_pos_k, cmp_w1_k, cmp_w2_k, idx), kn_cmp)
    vc = compress_blocks(kv(v_c), cmp_pos_v, cmp_w1_v, cmp_w2_v, idx)
    cmp_end = jnp.asarray(idx[:, -1])
    dist_c = pos[:, None] - cmp_end[None, :]
    s_c = jnp.einsum('bsgrd,bcgd->bgrsc', q, kc).astype(f32) - slopes[:, :, None, None] * dist_c
    any_valid = (pos >= CMP_LEN - 1).astype(f32)[:, None]
    p_c = jax.nn.softmax(jnp.where(dist_c >= 0, s_c, NEG), axis=-1) * any_valid
    o_cmp = jnp.einsum('bgrsc,bcgd->bsgrd', p_c.astype(vc.dtype), vc)

    n_sel = S // SLC_LEN
    imp = jnp.einsum('bgrsc,cj->bgsj', p_c, cmp_to_slc_overlap(n_cmp, n_sel))
    j = jnp.arange(n_sel)[None, :]
    blk_t = (pos // SLC_LEN)[:, None]
    imp = jnp.where(j <= blk_t, imp, NEG)
    imp = jnp.where((j == blk_t) | (j == 0), FORCE, imp)
    _, sel_idx = lax.top_k(imp, min(SLC_TOPK, n_sel))
    o_slc = selection_branch(q, rmsnorm(kv(k_s), kn_slc), kv(v_s), sel_idx, slopes)

    o_win = window_branch(q, rmsnorm(kv(k_w), kn_win), kv(v_w), slopes)

    gates = jax.nn.sigmoid(gate_logits.astype(f32)).reshape(B, S, G, R, 3)
    o = gates[..., 0:1] * o_cmp + gates[..., 1:2] * o_slc + gates[..., 2:3] * o_win
    return o.reshape(B, S, NSA_W).astype(q.dtype)


def setup_inputs(seed: int = 0) -> dict:
    key = jax.random.key(seed)
    ks = jax.random.split(key, 24)
    f32 = jnp.float32
    L = DEPTH

    def nrm(k, shape, scale):
        return jax.random.normal(k, shape, f32) * scale

    def gain(k, n):
        return 1.0 + 0.02 * jax.random.normal(k, (L, n), f32)

    dt = jnp.exp(jax.random.uniform(ks[4], (L, DN_HEADS), f32, math.log(1e-3), math.log(1e-1)))
    flat = CMP_LEN * NSA_HEAD_DIM
    return {
        "x": nrm(ks[0], (BATCH, SEQ, D_MODEL), 1.0),
        "norm_mix": gain(ks[1], D_MODEL),
        "w_in": nrm(ks[2], (L, D_MODEL, IN_WIDTH), D_MODEL ** -0.5),
        "dn_conv": nrm(ks[3], (L, DN_CONV, 3 * DN_W), DN_CONV ** -0.5),
        "dn_a_log": jnp.log(jax.random.uniform(ks[5], (L, DN_HEADS), f32, 1.0, 16.0)),
        "dn_dt_bias": dt + jnp.log(-jnp.expm1(-dt)),
        "dn_out_norm": gain(ks[6], DN_HEAD_DIM),
        "nsa_q_norm": gain(ks[7], NSA_HEAD_DIM),
        "nsa_k_norm_cmp": gain(ks[8], NSA_HEAD_DIM),
        "nsa_k_norm_slc": gain(ks[9], NSA_HEAD_DIM),
        "nsa_k_norm_win": gain(ks[10], NSA_HEAD_DIM),
        "cmp_pos_k": nrm(ks[11], (L, CMP_LEN, NSA_HEAD_DIM), 0.1),
        "cmp_w1_k": nrm(ks[12], (L, flat, CMP_HIDDEN), flat ** -0.5),
        "cmp_w2_k": nrm(ks[13], (L, CMP_HIDDEN, NSA_HEAD_DIM), CMP_HIDDEN ** -0.5),
        "cmp_pos_v": nrm(ks[14], (L, CMP_LEN, NSA_HEAD_DIM), 0.1),
        "cmp_w1_v": nrm(ks[15], (L, flat, CMP_HIDDEN), flat ** -0.5),
        "cmp_w2_v": nrm(ks[16], (L, CMP_HIDDEN, NSA_HEAD_DIM), CMP_HIDDEN ** -0.5),
        "w_proj_dn": nrm(ks[17], (L, DN_W, D_MODEL), DN_W ** -0.5),
        "w_proj_nsa": nrm(ks[18], (L, NSA_W, D_MODEL), NSA_W ** -0.5),
        "w_out": nrm(ks[19], (L, D_MODEL, D_MODEL), D_MODEL ** -0.5),
        "norm_mlp": gain(ks[20], D_MODEL),
        "w_up": nrm(ks[21], (L, D_MODEL, D_FF), D_MODEL ** -0.5),
        "w_down": nrm(ks[22], (L, D_FF, D_MODEL), D_FF ** -0.5),
    }


def reference(x, norm_mix, w_in, dn_conv, dn_a_log, dn_dt_bias, dn_out_norm,
              nsa_q_norm, nsa_k_norm_cmp, nsa_k_norm_slc, nsa_k_norm_win,
              cmp_pos_k, cmp_w1_k, cmp_w2_k, cmp_pos_v, cmp_w1_v, cmp_w2_v,
              w_proj_dn, w_proj_nsa, w_out, norm_mlp, w_up, w_down):
    splits = _in_split_points()
    for l in range(DEPTH):
        h = rmsnorm(x, norm_mix[l])
        (dn_qkv, dn_z, dn_b, dn_a, ns_q, ns_kc, ns_vc, ns_ks, ns_vs, ns_kw, ns_vw, ns_gate,
         gate_dn, gate_nsa) = jnp.split(h @ w_in[l], splits, axis=-1)
        o_dn = gated_deltanet(dn_qkv, dn_z, dn_b, dn_a, dn_conv[l], dn_a_log[l], dn_dt_bias[l], dn_out_norm[l])
        o_ns = nsa_mixer(ns_q, ns_kc, ns_vc, ns_ks, ns_vs, ns_kw, ns_vw, ns_gate,
                         nsa_q_norm[l], nsa_k_norm_cmp[l], nsa_k_norm_slc[l], nsa_k_norm_win[l],
                         cmp_pos_k[l], cmp_w1_k[l], cmp_w2_k[l], cmp_pos_v[l], cmp_w1_v[l], cmp_w2_v[l])
        mix = jax.nn.sigmoid(gate_dn) * (o_dn @ w_proj_dn[l]) + jax.nn.sigmoid(gate_nsa) * (o_ns @ w_proj_nsa[l])
        x = x + mix @ w_out[l]
        h2 = rmsnorm(x, norm_mlp[l])
        x = x + jnp.square(jax.nn.relu(h2 @ w_up[l])) @ w_down[l]
    return x
```

```python
import numpy as np
from contextlib import ExitStack
import concourse.bass as bass
import concourse.mybir as mybir
from concourse.bass_utils import run_bass_kernel_spmd

F32 = mybir.dt.float32
BF16 = mybir.dt.bfloat16
AF = mybir.ActivationFunctionType
ALU = mybir.AluOpType

N_DMA_SEMS = 24
T = 2048
D = 1024
NT = 16
EPS = 1e-6

C_QKV, C_Z, C_B, C_A, C_NQ, C_KC, C_VC, C_KS, C_VS, C_KW, C_VW, C_G, C_GDN, C_GNS = (
    0, 3072, 4096, 4104, 4112, 5136, 5392, 5648, 5904, 6160, 6416, 6672, 6720, 7744)


class Sched:
    def __init__(self, nc, stack, same_engine_waits=True):
        self.nc = nc
        self.engs = {"pe": nc.tensor, "act": nc.scalar, "dve": nc.vector,
                     "pool": nc.gpsimd, "sp": nc.sync}
        self.sem = {k: stack.enter_context(nc.semaphore("s_" + k)) for k in self.engs}
        self.cnt = {k: 0 for k in self.engs}
        self.dsem = [stack.enter_context(nc.semaphore("d%d" % i)) for i in range(N_DMA_SEMS)]
        self.dcnt = [0] * N_DMA_SEMS
        self.dnext = 0
        self.ops = {k: [] for k in self.engs}
        self.last_w = {}
        self.readers = {}
        self.waited = {k: {} for k in self.engs}
        self.same = same_engine_waits
        self.nops = 0

    def _semof(self, tag):
        return self.dsem[tag] if isinstance(tag, int) else self.sem[tag]

    def _deps(self, eng, reads, writes):
        deps = {}

        def need(tag, v):
            if deps.get(tag, 0) < v:
                deps[tag] = v
        for k in reads:
            w = self.last_w.get(k)
            if w is not None:
                need(*w)
        for k in writes:
            w = self.last_w.get(k)
            if w is not None:
                need(*w)
            for tag, v in self.readers.get(k, {}).items():
                need(tag, v)
        out = []
        for tag, v in deps.items():
            if tag == eng and (eng in ("pe", "sp") or not self.same):
                continue
            if self.waited[eng].get(tag, 0) >= v:
                continue
            self.waited[eng][tag] = v
            out.append((tag, v))
        return out

    def _emit_waits(self, eng, deps):
        for tag, v in deps:
            sem = self._semof(tag)
            self.ops[eng].append(lambda e, sem=sem, v=v: e.wait_ge(sem, v))

    def _commit(self, tag, v, reads, writes):
        for k in reads:
            self.readers.setdefault(k, {})[tag] = v
        for k in writes:
            self.last_w[k] = (tag, v)
            self.readers[k] = {}

    @staticmethod
    def _excl(reads, writes):
        isb = lambda k: isinstance(k, str) and k.startswith("bank")
        pr = [k for k in reads if isb(k)]
        if pr:
            reads = [k for k in reads if not isb(k)]
            writes = list(writes) + pr
        return reads, writes

    def op(self, eng, fn, reads=(), writes=()):
        reads, writes = self._excl(reads, writes)
        deps = self._deps(eng, reads, writes)
        self._emit_waits(eng, deps)
        self.cnt[eng] += 1
        v = self.cnt[eng]
        sem = self.sem[eng]
        self.ops[eng].append(lambda e, fn=fn, sem=sem: fn(e).then_inc(sem, 1))
        self._commit(eng, v, reads, writes)
        self.nops += 1

    def dma(self, q, out, in_, reads=(), writes=(), **kw):
        i = self.dnext
        self.dnext = (self.dnext + 1) % N_DMA_SEMS
        deps = self._deps(q, reads, writes)
        if self.dcnt[i] > 0 and self.waited[q].get(i, 0) < self.dcnt[i]:
            self.waited[q][i] = self.dcnt[i]
            deps.append((i, self.dcnt[i]))
        self._emit_waits(q, deps)
        self.dcnt[i] += 16
        v = self.dcnt[i]
        sem = self.dsem[i]
        self.ops[q].append(lambda e, out=out, in_=in_, sem=sem, kw=kw:
                           e.dma_start(out=out, in_=in_, **kw).then_inc(sem, 16))
        self._commit(i, v, reads, writes)
        self.nops += 1

    def barrier(self):
        for eng in self.engs:
            deps = []
            for tag in self.engs:
                if tag != eng and self.cnt[tag] > self.waited[eng].get(tag, 0):
                    self.waited[eng][tag] = self.cnt[tag]
                    deps.append((tag, self.cnt[tag]))
            for i in range(N_DMA_SEMS):
                if self.dcnt[i] > self.waited[eng].get(i, 0):
                    self.waited[eng][i] = self.dcnt[i]
                    deps.append((i, self.dcnt[i]))
            self._emit_waits(eng, deps)
        self.last_w = {}
        self.readers = {}

    def finish(self, keys):
        deps = self._deps("sp", keys, ())
        self._emit_waits("sp", deps)

    def run(self):
        nc = self.nc
        with nc.Block() as block:
            @block.sync
            def _(e):
                for f in self.ops["sp"]:
                    f(e)

            @block.tensor
            def _(e):
                for f in self.ops["pe"]:
                    f(e)

            @block.scalar
            def _(e):
                for f in self.ops["act"]:
                    f(e)

            @block.vector
            def _(e):
                for f in self.ops["dve"]:
                    f(e)

            @block.gpsimd
            def _(e):
                for f in self.ops["pool"]:
                    f(e)


class Arena:
    def __init__(self, A, ncols):
        self.A, self.n, self.top, self.marks = A, ncols, 0, []

    fallback = None

    def _take(self, cols):
        cols = (cols + 7) // 8 * 8
        if self.top + cols > self.n and self.fallback is not None:
            return self.fallback._take(cols)
        off = self.top
        self.top += cols
        assert self.top <= self.n, ("SBUF arena overflow", self.top, self.n)
        return off

    def f32(self, n):
        off = self._take(n)
        return self.A[:, off:off + n]

    def bf16(self, n):
        c = (n + 1) // 2
        off = self._take(c)
        return self.A[:, off:off + c].bitcast(BF16)[:, 0:n]

    def mark(self):
        self.marks.append(self.top)

    def release(self):
        self.top = self.marks.pop()


def build_nc(dbg=None):
    nc = bass.Bass("TRN2", target_bir_lowering=False)
    din = lambda name, shape: nc.dram_tensor(name, list(shape), F32, kind="ExternalInput").ap()
    x_d = din("x", [T, D])
    gmixB_d = din("gmixB", [128, D])
    gmlpB_d = din("gmlpB", [128, D])
    w_in_d = din("w_in", [D, 8768])
    w_dn_d = din("w_dn", [D, D])
    w_nsa_d = din("w_nsa", [D, D])
    w_out_d = din("w_out", [D, D])
    w_up_d = din("w_up", [D, 4096])
    w_down_d = din("w_down", [4096, D])
    ident_d = din("ident", [128, 128])
    cst_d = din("cst", [128, 5 * 128])
    convw_d = din("convw", [128, 96])
    gdnsm_d = din("gdnsm", [128, 3 * 128])
    nsm_d = din("nsm", [128, 8])
    maskc_d = din("maskc", [127, T])
    bbias_d = din("bbias", [128, 512])
    cmask_d = din("cmask", [128, 256])
    e2_d = din("e2", [32, T])
    krow_d = din("krow", [9, T + 127])
    qrow_d = din("qrow", [4, 9, 16 * 512])
    ov_d = din("ov", [127, 32])
    w1k_d = din("w1k", [2048, 128])
    w1v_d = din("w1v", [2048, 128])
    w2k_d = din("w2k", [128, 64])
    w2v_d = din("w2v", [128, 64])
    posk_d = din("posk", [64, 32])
    posv_d = din("posv", [64, 32])
    dbg_d = None
    if dbg:
        dbg_d = nc.dram_tensor("dbg", [128, 8 * T], F32, kind="ExternalOutput").ap()
    out_d = nc.dram_tensor("out", [T, D], F32, kind="ExternalOutput").ap()

    kchunk = lambda w, c0, c1: w[:, c0:c1].rearrange("(k p) n -> p k n", p=128)

    with ExitStack() as st:
        S = Sched(nc, st)
        A_t = st.enter_context(nc.sbuf_tensor("arena", [128, 53184], F32))
        AR = Arena(A_t, 53184)
        banks = [st.enter_context(nc.psum_tensor("bank%d" % i, [128, 512], F32)) for i in range(8)]
        bankbf = [b[:].bitcast(BF16) for b in banks]

        ident_b = AR.bf16(128)
        gmixB = AR.f32(D)
        gmlpB = AR.f32(D)
        hT = AR.bf16(8 * T).rearrange("p (k t) -> p k t", k=8)
        dead0 = AR.top
        o_dnT = AR.bf16(8 * T).rearrange("p (k t) -> p k t", k=8)
        o_nsT = AR.bf16(8 * T).rearrange("p (k t) -> p k t", k=8)
        dead1 = AR.top

        S.dma("pool", ident_b, ident_d, writes=["ident_b"])

        def mm(out, lhsT, rhs, reads, writes, start=True, stop=True):
            S.op("pe", lambda e: e.matmul(out, lhsT=lhsT, rhs=rhs, start=start, stop=stop), reads=reads, writes=writes)

        def tp(out, in_, ident, reads, writes):
            S.op("pe", lambda e: e.transpose(out=out, in_=in_, identity=ident), reads=reads, writes=writes)

        def act(out, in_, func, reads, writes, **kw):
            S.op("act", lambda e: e.activation(out=out, in_=in_, func=func, **kw), reads=reads, writes=writes)

        def tt(eng, out, in0, in1, op, reads, writes):
            S.op(eng, lambda e: e.tensor_tensor(out=out, in0=in0, in1=in1, op=op), reads=reads, writes=writes)

        def ts(eng, out, in0, s1, op0, reads, writes, s2=None, op1=None):
            if op1 is None:
                S.op(eng, lambda e: e.tensor_scalar(out=out, in0=in0, scalar1=s1, scalar2=None, op0=op0),
                     reads=reads, writes=writes)
            else:
                S.op(eng, lambda e: e.tensor_scalar(out=out, in0=in0, scalar1=s1, scalar2=s2, op0=op0, op1=op1),
                     reads=reads, writes=writes)

        def stt(out, in0, scalar, in1, op0, op1, reads, writes):
            S.op("dve", lambda e: e.scalar_tensor_tensor(out=out, in0=in0, scalar=scalar, in1=in1, op0=op0, op1=op1),
                 reads=reads, writes=writes)

        def cp(eng, out, in_, reads, writes):
            if eng == "act":
                S.op("act", lambda e: e.copy(out=out, in_=in_), reads=reads, writes=writes)
            else:
                S.op(eng, lambda e: e.tensor_copy(out=out, in_=in_), reads=reads, writes=writes)
        S.dma("sp", gmixB, gmixB_d, writes=["gmixB"])
        S.dma("sp", gmlpB, gmlpB_d, writes=["gmlpB"])

        def rmsnorm_to_T(src_tile, src_key, gB, gkey, t, scr, dstT, dst_key_fn, tag):
            junk, ss, rstd, hb = scr
            S.op("act", lambda e: e.activation(out=junk, in_=src_tile, func=AF.Square, accum_out=ss),
                 reads=[src_key], writes=[tag + "junk", tag + "ss"])
            S.op("act", lambda e: e.activation(out=rstd, in_=ss, func=AF.Ln, scale=1.0 / D, bias=EPS),
                 reads=[tag + "ss"], writes=[tag + "rstd"])
            S.op("act", lambda e: e.activation(out=rstd, in_=rstd, func=AF.Exp, scale=-0.5),
                 reads=[tag + "rstd"], writes=[tag + "rstd"])
            S.op("dve", lambda e: e.scalar_tensor_tensor(out=hb, in0=src_tile, scalar=rstd, in1=gB,
                                                         op0=ALU.mult, op1=ALU.mult),
                 reads=[src_key, tag + "rstd", gkey], writes=[tag + "hb"])
            pb = bankbf[7]
            for k in range(8):
                S.op("pe", lambda e, k=k: e.transpose(out=pb[:, k * 128:(k + 1) * 128],
                                                      in_=hb[:, k * 128:(k + 1) * 128], identity=ident_b),
                     reads=[tag + "hb", "ident_b"], writes=["bank7"])
            S.op("act", lambda e: e.copy(out=dstT[:, :, t * 128:(t + 1) * 128],
                                         in_=pb.rearrange("p (k t) -> p k t", k=8)),
                 reads=["bank7"], writes=[dst_key_fn(t)])

        AR.mark()
        xt = [AR.f32(D) for _ in range(2)]
        scr = [(AR.bf16(D), AR.f32(1), AR.f32(1), AR.bf16(D)) for _ in range(2)]
        for t in range(NT):
            b = t % 2
            S.dma("sp", xt[b], x_d[t * 128:(t + 1) * 128, :], writes=["xt%d" % b])
            rmsnorm_to_T(xt[b], "xt%d" % b, gmixB, "gmixB", t, scr[b], hT,
                         lambda t: ("hT", t // 4), "p0_%d" % b)
        S.barrier()
        AR.release()

        NH = 2
        AR.mark()
        ARs = Arena(A_t, dead1)
        ARs.top = dead0 + 8192
        ARs.fallback = AR
        cst = AR.f32(640)
        ones_f, Ltri, maskA, maskQ, ident_f = [cst[:, i * 128:(i + 1) * 128] for i in range(5)]
        convw = AR.f32(96)
        gdnsm = AR.f32(384)
        alog_t, dtb_t, gdn_p = gdnsm[:, 0:128], gdnsm[:, 128:256], gdnsm[:, 256:257]
        ones_b = AR.bf16(128)
        S.dma("sp", cst, cst_d, writes=["cst"])
        S.dma("sp", convw, convw_d, writes=["convw"])
        S.dma("sp", gdnsm, gdnsm_d, writes=["gdnsm"])
        S.op("pool", lambda e: e.memset(ones_b, 1.0), writes=["ones_b"])
        beta, gg, gcs, egc, bg, eend, egl, nbeta, xa, ea = [AR.f32(128) for _ in range(10)]
        wba = AR.bf16(8 * 16).rearrange("p (k n) -> p k n", k=8)
        S.dma("pool", wba, kchunk(w_in_d, C_B, C_B + 16), writes=["wba"])
        for t in range(NT):
            for k in range(8):
                mm(banks[0][:, t * 16:(t + 1) * 16], hT[:, k, t * 128:(t + 1) * 128], wba[:, k, :],
                   ["wba"], ["bank0"], start=(k == 0), stop=(k == 7))
        ba3 = banks[0][:, 0:256].rearrange("p (t c) -> p t c", t=16)
        v3 = lambda a: a.rearrange("p (t h) -> p t h", t=16)
        act(v3(beta), ba3[:, :, 0:8], AF.Sigmoid, ["bank0"], ["beta"])
        tt("dve", v3(xa), ba3[:, :, 8:16], v3(dtb_t), ALU.add, ["bank0", "gdnsm"], ["xa"])
        act(xa, xa, AF.Exp, ["xa"], ["xa"])
        act(xa, xa, AF.Ln, ["xa"], ["xa"], bias=1.0)
        act(ea, alog_t, AF.Exp, ["gdnsm"], ["ea"])
        stt(gg, xa, -1.0, ea, ALU.mult, ALU.mult, ["xa", "ea"], ["gg"])
        mm(banks[1][:, 0:128], Ltri, gg, ["cst", "gg"], ["bank1"])
        mm(banks[1][:, 128:256], ones_f, gg, ["cst", "gg"], ["bank1"])
        cp("act", gcs, banks[1][:, 0:128], ["bank1"], ["gcs"])
        act(egc, banks[1][:, 0:128], AF.Exp, ["bank1"], ["egc"])
        tt("dve", bg, beta, egc, ALU.mult, ["beta", "egc"], ["bg"])
        tt("dve", eend, banks[1][:, 128:256], gcs, ALU.subtract, ["bank1", "gcs"], ["eend"])
        act(eend, eend, AF.Exp, ["eend"], ["eend"])
        act(egl, banks[1][:, 128:256], AF.Exp, ["bank1"], ["egl"])
        ts("dve", nbeta, beta, -1.0, ALU.mult, ["beta"], ["nbeta"])
        S.barrier()

        v16 = lambda a: a.rearrange("p (c n) -> p c n", c=16)
        pers = [[v16(AR.bf16(T)), v16(AR.f32(T)), v16(AR.bf16(T)), v16(AR.bf16(T)), v16(AR.bf16(T))]
                for _ in range(NH)]
        pre = ARs.bf16(3 * 2052).rearrange("p (j n) -> p j n", j=3)
        S.op("pool", lambda e: e.memset(pre[:, :, 0:4], 0.0), writes=["pre"])
        wqkv = ARs.bf16(8 * 384).rearrange("p (k n) -> p k n", k=8)
        wz = [ARs.bf16(8 * 128).rearrange("p (k n) -> p k n", k=8) for _ in range(NH)]
        diag = ARs.bf16(12 * 128).rearrange("p (j n) -> p j n", j=12)
        s_f = [ARs.f32(512) for _ in range(2)]
        sqb = [ARs.bf16(512) for _ in range(2)]
        r_f = [ARs.f32(512) for _ in range(2)]
        qkvT = [[ARs.bf16(512) for _ in range(3)] for _ in range(2)]
        kbg4 = ARs.bf16(512)
        vb4 = ARs.bf16(512)
        Lg4 = ARs.f32(512)
        tmp4 = ARs.f32(512)
        D24 = ARs.f32(512)
        D34 = ARs.f32(512)
        EG4 = ARs.f32(512)
        RLb = [AR.bf16(1024).rearrange("p (a n) -> p a n", a=2) for _ in range(2)]
        Rc = [RLb[i][:, 0, :] for i in range(2)]
        Lc = [RLb[i][:, 1, :] for i in range(2)]
        Xc = [AR.bf16(512) for _ in range(2)]
        L0k = AR.bf16(512)
        L0l = AR.bf16(512)
        Rz4 = AR.bf16(512)
        XT4 = AR.bf16(512)
        Sst = [AR.f32(128) for _ in range(NH)]
        Sbf = [AR.bf16(128) for _ in range(NH)]
        Slo = [AR.bf16(128) for _ in range(NH)]
        vnew = [AR.bf16(128) for _ in range(NH)]
        oraw = [AR.f32(512) for _ in range(NH)]
        sqo = AR.bf16(512)
        ro = AR.f32(512)
        szo = AR.f32(512)
        ono = AR.f32(512)
        trb = bankbf[7]

        def interleave(*gens):
            gens = [g_ for g_ in gens if g_ is not None]
            while gens:
                for g_ in list(gens):
                    try:
                        next(g_)
                    except StopIteration:
                        gens.remove(g_)

        def chain_gens(*gens):
            for g_ in gens:
                if g_ is not None:
                    yield from g_

        b4 = lambda a: a.rearrange("p (t n) -> p t n", t=4)

        def head_setup(h, hi):
            for j in range(3):
                S.dma("pool", wqkv[:, :, j * 128:(j + 1) * 128],
                      kchunk(w_in_d, C_QKV + j * 1024 + h * 128, C_QKV + j * 1024 + (h + 1) * 128), writes=[("wqkv", j)])
            S.dma("pool", wz[hi], kchunk(w_in_d, C_Z + h * 128, C_Z + (h + 1) * 128), writes=["wz%d" % hi])
            for j in range(3):
                for tap in range(4):
                    ci = (j * 8 + h) * 4 + tap
                    ts("dve", diag[:, j * 4 + tap, :], ident_b, convw[:, ci:ci + 1], ALU.mult,
                       ["ident_b", "convw"], [("diag", j)])
            yield

        def conv_stage(h, hi, cc):
            ts_ = slice(cc * 512, (cc + 1) * 512)
            par = cc % 2
            qT_, kT_, vT_ = qkvT[par]
            for j in range(3):
                pb = banks[0]
                for k in range(8):
                    mm(pb[:], wqkv[:, k, j * 128:(j + 1) * 128], hT[:, k, ts_], [("wqkv", j)], ["bank0"],
                       start=(k == 0), stop=(k == 7))
                cp("dve" if j != 1 else "act", pre[:, j, 4 + cc * 512:4 + (cc + 1) * 512], pb[:], ["bank0"],
                   [("pre", j)])
                yield
            for j in range(3):
                pb = banks[1]
                bk = "bank1"
                for tap in range(4):
                    mm(pb[:], diag[:, j * 4 + tap, :], pre[:, j, 1 + cc * 512 + tap:1 + cc * 512 + tap + 512],
                       [("diag", j), ("pre", j), "pre"], [bk], start=(tap == 0), stop=(tap == 3))
                if j == 2:
                    act(vT_, pb[:], AF.Silu, [bk], ["vT%d" % par])
                else:
                    act(s_f[j], pb[:], AF.Silu, [bk], ["s_f%d" % j])
                    tt("pool", sqb[j], s_f[j], s_f[j], ALU.mult, ["s_f%d" % j], ["sqb%d" % j])
                yield
            for j in range(2):
                mm(banks[1][:], ones_b, sqb[j], ["ones_b", "sqb%d" % j], ["bank1"])
                act(r_f[j], banks[1][:], AF.Ln, ["bank1"], ["r_f%d" % j], bias=EPS)
                yield
            for j in range(2):
                act(r_f[j], r_f[j], AF.Exp, ["r_f%d" % j], ["r_f%d" % j], scale=-0.5)
                if j == 0:
                    stt(qT_, s_f[0], 128 ** -0.5, r_f[0], ALU.mult, ALU.mult, ["s_f0", "r_f0"], ["qT%d" % par])
                else:
                    tt("dve", kT_, s_f[1], r_f[1], ALU.mult, ["s_f1", "r_f1"], ["kT%d" % par])
                yield

        def tile_stage(h, hi, cc):
            WT, U, Kend, Qdec, QKm = pers[hi]
            P = "h%d_" % hi
            par = cc % 2
            qT_, kT_, vT_ = qkvT[par]
            qk, kk, vk = "qT%d" % par, "kT%d" % par, "vT%d" % par
            c0 = cc * 4
            cs = slice(c0, c0 + 4)
            col4 = lambda a: a.rearrange("p (t h) -> p t h", t=16)[:, c0:c0 + 4, h]
            bc = lambda a: col4(a).unsqueeze(2).broadcast_to([128, 4, 128])
            bm = lambda a: a.unsqueeze(1).broadcast_to([128, 4, 128])
            tsl = lambda tl: slice(tl * 128, (tl + 1) * 128)
            for tl in range(4):
                tp(trb[:, tsl(tl)], kT_[:, tsl(tl)], ident_b, [kk, "ident_b"], ["bank7"])
            for tl in range(4):
                tp(trb[:, 512 + tl * 128:512 + (tl + 1) * 128], vT_[:, tsl(tl)], ident_b, [vk, "ident_b"], ["bank7"])
            tt("dve", Kend[:, cs, :], b4(trb[:, 0:512]), bc(eend), ALU.mult, ["bank7", "eend"], [(P + "Kend", cc)])
            tt("dve", b4(kbg4), b4(trb[:, 0:512]), bc(bg), ALU.mult, ["bank7", "bg"], ["kbg4"])
            tt("dve", b4(vb4), b4(trb[:, 512:1024]), bc(beta), ALU.mult, ["bank7", "beta"], ["vb4"])
            yield
            tt("pool", b4(Lg4), bm(Ltri), bc(gg), ALU.mult, ["cst", "gg"], ["Lg4"])
            mm(banks[5][:], ones_f, Lg4, ["cst", "Lg4"], ["bank5"])
            tt("dve", b4(tmp4), b4(banks[5][:]), bc(gcs), ALU.subtract, ["bank5", "gcs"], ["tmp4"])
            act(EG4, banks[5][:], AF.Exp, ["bank5"], ["EG4"])
            yield
            for tl in range(4):
                mm(banks[6][:, tsl(tl)], kT_[:, tsl(tl)], kT_[:, tsl(tl)], [kk], ["bank6"])
            tt("dve", b4(D24), b4(tmp4), bm(maskA), ALU.max, ["tmp4", "cst"], ["D24"])
            tt("dve", b4(D34), b4(tmp4), bm(maskQ), ALU.min, ["tmp4", "cst"], ["D34"])
            act(D24, D24, AF.Exp, ["D24"], ["D24"], scale=-1.0)
            tt("pool", b4(D24), b4(D24), bc(nbeta), ALU.mult, ["D24", "nbeta"], ["D24"])
            RLk = lambda i: [("RL", i, 0), ("RL", i, 1)]
            Xk = lambda i: [("X", i, 0), ("X", i, 1)]
            tt("dve", tmp4, banks[6][:], D24, ALU.mult, ["bank6", "D24"], ["tmp4"])
            cp("act", Lc[0], tmp4, ["tmp4"], RLk(0))
            tt("dve", L0l, tmp4, Lc[0], ALU.subtract, ["tmp4"] + RLk(0), ["L0l"])
            cp("pool", L0k, Lc[0], RLk(0), ["L0k"])
            yield
            for tl in range(4):
                mm(banks[5][:, tsl(tl)], kT_[:, tsl(tl)], qT_[:, tsl(tl)], [kk, qk], ["bank5"])
            act(D34, D34, AF.Exp, ["D34"], ["D34"])
            tt("dve", QKm[:, cs, :], b4(banks[5][:]), b4(D34), ALU.mult, ["bank5", "D34"], [(P + "QKm", cc)])
            tt("pool", Qdec[:, cs, :], b4(qT_), b4(EG4), ALU.mult, [qk, "EG4"], [(P + "Qdec", cc)])
            yield
            for tl in range(4):
                tp(trb[:, tsl(tl)], Lc[0][:, tsl(tl)], ident_b, RLk(0) + ["ident_b"], ["bank7"])
            cp("act", Rc[0], trb[:, 0:512], ["bank7"], RLk(0))
            tt("pool", b4(Xc[0]), b4(Rc[0]), bm(ident_b), ALU.add, RLk(0) + ["ident_b"], Xk(0))
            yield
            cur = 0
            pairs = [(0, 5, 7, "act"), (1, 6, 4, "dve")]
            for lv in range(1, 7):
                nx = 1 - cur
                for (pi, rlb, xb, ce) in pairs:
                    for t2 in range(2):
                        tl = pi * 2 + t2
                        if lv < 6:
                            mm(banks[rlb][:, t2 * 128:(t2 + 1) * 128], Lc[cur][:, tsl(tl)], Rc[cur][:, tsl(tl)],
                               [("RL", cur, pi)], ["bank%d" % rlb])
                        mm(banks[rlb][:, 256 + t2 * 128:256 + (t2 + 1) * 128], Rc[cur][:, tsl(tl)], Lc[cur][:, tsl(tl)],
                           [("RL", cur, pi)], ["bank%d" % rlb])
                for (pi, rlb, xb, ce) in pairs:
                    psl = slice(pi * 256, (pi + 1) * 256)
                    if lv < 6:
                        cp(ce, RLb[nx][:, :, psl], banks[rlb][:].rearrange("p (a n) -> p a n", a=2), ["bank%d" % rlb],
                           [("RL", nx, pi)])
                    else:
                        cp(ce, Lc[nx][:, psl], banks[rlb][:, 256:512], ["bank%d" % rlb], [("RL", nx, pi)])
                for (pi, rlb, xb, ce) in pairs:
                    for t2 in range(2):
                        tl = pi * 2 + t2
                        mm(banks[xb][:, t2 * 128:(t2 + 1) * 128], Lc[nx][:, tsl(tl)], Xc[cur][:, tsl(tl)],
                           [("RL", nx, pi), ("X", cur, pi)], ["bank%d" % xb])
                for (pi, rlb, xb, ce) in pairs:
                    psl = slice(pi * 256, (pi + 1) * 256)
                    tt("dve", Xc[nx][:, psl], banks[xb][:, 0:256], Xc[cur][:, psl], ALU.add,
                       ["bank%d" % xb, ("X", cur, pi)], [("X", nx, pi)])
                cur = nx
                yield
            X0 = Xc[cur]
            X1 = Xc[1 - cur]
            for tl in range(4):
                mm(banks[5][:, tsl(tl)], L0k[:, tsl(tl)], X0[:, tsl(tl)], ["L0k"] + Xk(cur), ["bank5"], start=True, stop=False)
                mm(banks[5][:, tsl(tl)], L0l[:, tsl(tl)], X0[:, tsl(tl)], ["L0l"] + Xk(cur), ["bank5"], start=False, stop=True)
            for tl in range(4):
                tp(trb[:, tsl(tl)], X0[:, tsl(tl)], ident_b, Xk(cur) + ["ident_b"], ["bank7"])
            tt("dve", EG4, banks[5][:], X0, ALU.subtract, ["bank5", "EG4"] + Xk(cur), ["EG4"])
            cp("act", XT4, trb[:, 0:512], ["bank7"], ["XT4"])
            tt("dve", b4(Rz4), b4(EG4), bm(ident_b), ALU.add, ["EG4", "ident_b"], ["Rz4"])
            yield
            for tl in range(4):
                mm(banks[6][:, tsl(tl)], XT4[:, tsl(tl)], Rz4[:, tsl(tl)], ["XT4", "Rz4"], ["bank6"])
            tt("dve", X1, banks[6][:], X0, ALU.add, ["bank6"] + Xk(cur), Xk(1 - cur))
            cur = 1 - cur
            yield
            Xf = Xc[cur]
            for tl in range(4):
                mm(banks[5][:, tsl(tl)], Xf[:, tsl(tl)], vb4[:, tsl(tl)], Xk(cur) + ["vb4"], ["bank5"])
            for tl in range(4):
                mm(banks[6][:, tsl(tl)], kbg4[:, tsl(tl)], Xf[:, tsl(tl)], Xk(cur) + ["kbg4"], ["bank6"])
            cp("act", U[:, cs, :], b4(banks[5][:]), ["bank5"], [(P + "U", cc)])
            cp("dve", WT[:, cs, :], b4(banks[6][:]), ["bank6"], [(P + "WT", cc)])
            yield

        def scan_gen(h, hi, cc):
            WT, U, Kend, Qdec, QKm = pers[hi]
            P = "h%d_" % hi
            if cc == 0:
                S.op("pool", lambda e: e.memset(Sst[hi], 0.0), writes=["S%d" % hi])
                S.op("pool", lambda e: e.memset(Sbf[hi], 0.0), writes=["Sbf%d" % hi])
                S.op("pool", lambda e: e.memset(Slo[hi], 0.0), writes=["Slo%d" % hi])
            for c in range(4 * cc, 4 * cc + 4):
                col = c * 8 + h
                mm(banks[2][:, 0:128], WT[:, c, :], Sbf[hi], [(P + "WT", cc), "Sbf%d" % hi], ["bank2"], start=True, stop=False)
                mm(banks[2][:, 0:128], WT[:, c, :], Slo[hi], [(P + "WT", cc), "Slo%d" % hi], ["bank2"], start=False, stop=True)
                tt("dve", vnew[hi], U[:, c, :], banks[2][:, 0:128], ALU.subtract, [(P + "U", cc), "bank2"], ["vnew%d" % hi])
                yield
                mm(banks[3][:, 0:128], Sbf[hi], Qdec[:, c, :], ["Sbf%d" % hi, (P + "Qdec", cc)], ["bank3"], start=True, stop=False)
                mm(banks[3][:, 0:128], Slo[hi], Qdec[:, c, :], ["Slo%d" % hi, (P + "Qdec", cc)], ["bank3"], start=False, stop=False)
                mm(banks[3][:, 0:128], vnew[hi], QKm[:, c, :], ["vnew%d" % hi, (P + "QKm", cc)], ["bank3"], start=False, stop=True)
                mm(banks[2][:, 128:256], Kend[:, c, :], vnew[hi], [(P + "Kend", cc), "vnew%d" % hi], ["bank2"])
                stt(Sst[hi], Sst[hi], egl[:, col:col + 1], banks[2][:, 128:256], ALU.mult, ALU.add,
                    ["S%d" % hi, "egl", "bank2"], ["S%d" % hi])
                cp("act", Sbf[hi], Sst[hi], ["S%d" % hi], ["Sbf%d" % hi])
                tt("dve", Slo[hi], Sst[hi], Sbf[hi], ALU.subtract, ["S%d" % hi, "Sbf%d" % hi], ["Slo%d" % hi])
                cp("act", oraw[hi][:, (c % 4) * 128:(c % 4 + 1) * 128], banks[3][:, 0:128], ["bank3"], ["oraw%d" % hi])
                yield
            ts_ = slice(cc * 512, (cc + 1) * 512)
            tt("pool", sqo, oraw[hi], oraw[hi], ALU.mult, ["oraw%d" % hi], ["sqo"])
            mm(banks[3][:], ones_b, sqo, ["ones_b", "sqo"], ["bank3"])
            act(ro, banks[3][:], AF.Ln, ["bank3"], ["ro"], scale=1.0 / 128, bias=EPS)
            act(ro, ro, AF.Exp, ["ro"], ["ro"], scale=-0.5)
            yield
            for k in range(8):
                mm(banks[2][:], wz[hi][:, k, :], hT[:, k, ts_], ["wz%d" % hi], ["bank2"], start=(k == 0), stop=(k == 7))
            act(szo, banks[2][:], AF.Silu, ["bank2"], ["szo"])
            tt("dve", ono, oraw[hi], ro, ALU.mult, ["oraw%d" % hi, "ro"], ["ono"])
            stt(o_dnT[:, h, ts_], ono, gdn_p, szo, ALU.mult, ALU.mult, ["ono", "gdnsm", "szo"], [("o_dnT", h)])
            yield

        interleave(head_setup(0, 0))
        interleave(conv_stage(0, 0, 0))
        pend_scan = None
        for h in range(8):
            hi = h % NH
            for cc in range(4):
                if cc < 3:
                    other = conv_stage(h, hi, cc + 1)
                elif h + 1 < 8:
                    other = chain_gens(head_setup(h + 1, (h + 1) % NH), conv_stage(h + 1, (h + 1) % NH, 0))
                else:
                    other = None
                interleave(tile_stage(h, hi, cc), other, pend_scan)
                pend_scan = scan_gen(h, hi, cc)
        interleave(pend_scan)
        S.barrier()
        AR.release()
        if dbg == "gdn":
            dsc = AR.f32(T)
            for k in range(8):
                cp("dve", dsc, o_dnT[:, k, :], [], ["dsc"])
                S.dma("sp", dbg_d[:, k * T:(k + 1) * T], dsc, reads=["dsc"], writes=[("dbg", k)])
            S.finish([("dbg", k) for k in range(8)])
            S.run()
            return nc

        AR.mark()
        v4 = lambda a: a.rearrange("p (q r n) -> p q r n", q=16, r=4)
        onesb2 = AR.bf16(128)
        S.op("pool", lambda e: e.memset(onesb2, 1.0), writes=["onesb2"])
        Qg = AR.bf16(16 * 512)
        Qg4 = v4(Qg)
        KS = AR.bf16(T)
        KW = AR.bf16(T)
        KC = [AR.bf16(128) for _ in range(4)]
        VC = [AR.bf16(104) for _ in range(4)]
        VS = AR.bf16(16 * 4 * 66).rearrange("p (k g n) -> p k g n", k=16, g=4)
        VW = AR.bf16(16 * 4 * 66).rearrange("p (k g n) -> p k g n", k=16, g=4)
        maskc = AR.bf16(T)
        bbias = AR.f32(512).rearrange("p (q j) -> p q j", q=16)
        cmask = AR.f32(256)
        causal, anti = cmask[:, 0:128], cmask[:, 128:256]
        gates = AR.f32(768).rearrange("p (q g r x) -> p q g r x", q=16, g=4, r=4)
        nsm = AR.f32(8)
        S.dma("sp", nsm, nsm_d, writes=["nsm"])
        S.dma("pool", maskc[0:127, :], maskc_d, writes=["maskc"])
        S.dma("sp", bbias, bbias_d.rearrange("p (q j) -> p q j", q=16), writes=["bbias"])
        S.dma("sp", cmask, cmask_d, writes=["cmask"])
        S.op("pool", lambda e: e.memset(Qg[64:96, :], 0.0), writes=["Qg_sel_all"])
        S.op("pool", lambda e: e.memset(KW[64:96, :], 0.0), writes=["KWc"])
        S.dma("pool", KS[64:96, :], e2_d, writes=["KSc"])
        S.dma("pool", KS[96:105, :], krow_d[:, 0:T], writes=["KSc"])
        S.dma("pool", KW[96:105, :], krow_d[:, 0:T], writes=["KWc"])
        for g in range(4):
            S.op("pool", lambda e, g=g: e.memset(KC[g][64:96, :], 0.0), writes=[("KCc", g)])
            S.dma("pool", KC[g][96:105, 0:127], krow_d[:, T:T + 127], writes=[("KCc", g)])
            S.dma("pool", VC[g][0:127, 65:97], ov_d, writes=[("VCc", g)])
            S.op("pool", lambda e, g=g: e.memset(VC[g][:, 64:65], 1.0), writes=[("VCc", g)])
        S.op("pool", lambda e: e.memset(VS[:, :, :, 64:65], 1.0), writes=["VSc"])
        S.op("pool", lambda e: e.memset(VW[:, :, :, 64:65], 1.0), writes=["VWc"])

        AR.mark()
        wv = AR.bf16(8 * 512).rearrange("p (k n) -> p k n", k=8)
        S.dma("pool", wv[:, :, 0:256], kchunk(w_in_d, C_VS, C_VS + 256), writes=["wv"])
        S.dma("pool", wv[:, :, 256:512], kchunk(w_in_d, C_VW, C_VW + 256), writes=["wv"])
        wg = AR.bf16(8 * 48).rearrange("p (k n) -> p k n", k=8)
        S.dma("pool", wg, kchunk(w_in_d, C_G, C_G + 48), writes=["wg"])
        def vproj_gen():
            for t in range(NT):
                tsl = slice(t * 128, (t + 1) * 128)
                for k in range(8):
                    mm(banks[2][:], hT[:, k, tsl], wv[:, k, :], ["wv"], ["bank2"], start=(k == 0), stop=(k == 7))
                pv = banks[2][:].rearrange("p (a g n) -> p a g n", a=2, g=4)
                cp("act", VS[:, t, :, 0:64], pv[:, 0], ["bank2", "VSc"], [("VS", t)])
                cp("dve", VW[:, t, :, 0:64], pv[:, 1], ["bank2", "VWc"], [("VW", t)])
                for k in range(8):
                    mm(banks[3][:, 0:48], hT[:, k, tsl], wg[:, k, :], ["wg"], ["bank3"], start=(k == 0), stop=(k == 7))
                act(gates[:, t].rearrange("p g r x -> p (g r x)"), banks[3][:, 0:48], AF.Sigmoid, ["bank3"], [("gates", t)])
                yield

        def take(gen, n):
            for _ in range(n):
                try:
                    next(gen)
                except StopIteration:
                    return
                yield

        w1 = [AR.bf16(32 * 128).rearrange("p (l j) -> p l j", l=32) for _ in range(2)]
        w2 = [AR.bf16(64) for _ in range(2)]
        posT = [AR.bf16(32) for _ in range(2)]
        bj = [AR.f32(1) for _ in range(2)]
        for i, (w1d, w2d, pd) in enumerate([(w1k_d, w2k_d, posk_d), (w1v_d, w2v_d, posv_d)]):
            S.dma("pool", w1[i][0:64], w1d.rearrange("(l d) j -> d l j", d=64), writes=[("w1", i)])
            S.dma("pool", w2[i], w2d, writes=[("w2", i)])
            S.dma("pool", posT[i][0:64], pd, writes=[("posT", i)])
            for l in range(32):
                mm(banks[4][:, 0:1], w1[i][0:64, l, :], posT[i][0:64, l:l + 1], [("w1", i), ("posT", i)], ["bank4"],
                   start=(l == 0), stop=(l == 31))
            cp("act", bj[i], banks[4][:, 0:1], ["bank4"], [("bj", i)])
        wkv = [AR.bf16(8 * 128).rearrange("p (k n) -> p k n", k=8) for _ in range(2)]
        craw = [AR.bf16(T + 16) for _ in range(2)]
        xh = [AR.f32(128) for _ in range(2)]
        x2 = [AR.f32(128) for _ in range(2)]
        sgm = [AR.f32(128) for _ in range(2)]
        gl_ = [AR.bf16(128) for _ in range(2)]
        kcf = AR.f32(128)
        kcs = AR.bf16(128)
        kcr = AR.f32(128)

        def wkv_load(g):
            S.dma("pool", wkv[g % 2][:, :, 0:64], kchunk(w_in_d, C_KC + g * 64, C_KC + (g + 1) * 64), writes=[("wkv", g % 2)])
            S.dma("pool", wkv[g % 2][:, :, 64:128], kchunk(w_in_d, C_VC + g * 64, C_VC + (g + 1) * 64), writes=[("wkv", g % 2)])

        def cmp_gen(g, i):
            pbn, hbn, obn = i, 4 + 2 * i, 5 + 2 * i
            pb, hb, ob = banks[pbn], banks[hbn], banks[obn]
            pk, hk, ok = "bank%d" % pbn, "bank%d" % hbn, "bank%d" % obn
            X, X2, SG, GL = xh[i], x2[i], sgm[i], gl_[i]
            for tc in range(4):
                for k in range(8):
                    mm(pb[0:64, :], wkv[g % 2][:, k, i * 64:(i + 1) * 64], hT[:, k, tc * 512:(tc + 1) * 512], [("wkv", g % 2)], [pk],
                       start=(k == 0), stop=(k == 7))
                cp("act" if tc % 2 else "dve", craw[i][0:64, tc * 512:(tc + 1) * 512], pb[0:64, :], [pk], [("craw", i)])
                yield
            c3 = craw[i][0:64, 0:T].rearrange("p (n s) -> p n s", s=16)
            for l in range(32):
                rhs = c3[:, 0:127, l] if l < 16 else c3[:, 1:128, l - 16]
                mm(hb[:, 0:127], w1[i][0:64, l, :], rhs, [("w1", i), ("craw", i)], [hk], start=(l == 0), stop=(l == 31))
                if l % 8 == 7:
                    yield
            act(X[:, 0:127], hb[:, 0:127], AF.Identity, [hk, ("bj", i)], ["xh%d" % i], bias=bj[i])
            tt("pool", X2[:, 0:127], X[:, 0:127], X[:, 0:127], ALU.mult, ["xh%d" % i], ["x2%d" % i])
            yield
            ts("dve", X2[:, 0:127], X2[:, 0:127], 0.044715, ALU.mult, ["x2%d" % i], ["x2%d" % i], s2=1.0, op1=ALU.add)
            tt("dve", X2[:, 0:127], X2[:, 0:127], X[:, 0:127], ALU.mult, ["x2%d" % i, "xh%d" % i], ["x2%d" % i])
            act(SG[:, 0:127], X2[:, 0:127], AF.Sigmoid, ["x2%d" % i], ["sgm%d" % i], scale=1.5957691216057308)
            tt("dve", GL[:, 0:127], X[:, 0:127], SG[:, 0:127], ALU.mult, ["xh%d" % i, "sgm%d" % i], ["gl%d" % i])
            yield
            if i == 0:
                mm(ob[0:64, 0:127], w2[0], GL[:, 0:127], [("w2", 0), "gl0"], [ok])
                cp("act", kcf[0:64, 0:127], ob[0:64, 0:127], [ok], ["kcf"])
                tt("pool", kcs[0:64, 0:127], kcf[0:64, 0:127], kcf[0:64, 0:127], ALU.mult, ["kcf"], ["kcs"])
                yield
                mm(ob[0:64, 128:255], onesb2[0:64, 0:64], kcs[0:64, 0:127], ["onesb2", "kcs"], [ok])
                act(kcr[0:64, 0:127], ob[0:64, 128:255], AF.Ln, [ok], ["kcr"], scale=1.0 / 64, bias=EPS)
                act(kcr[0:64, 0:127], kcr[0:64, 0:127], AF.Exp, ["kcr"], ["kcr"], scale=-0.5)
                stt(KC[g][0:64, 0:127], kcf[0:64, 0:127], nsm[0:64, 1:2], kcr[0:64, 0:127], ALU.mult, ALU.mult,
                    ["kcf", "nsm", "kcr"], [("KC", g)])
            else:
                mm(ob[0:127, 256:320], GL[:, 0:127], w2[1], [("w2", 1), "gl1"], [ok])
                cp("act", VC[g][0:127, 0:64], ob[0:127, 256:320], [ok], [("VC", g)])
            yield

        vp = vproj_gen()
        wkv_load(0)
        for g in range(4):
            if g + 1 < 4:
                wkv_load(g + 1)
            interleave(cmp_gen(g, 0), cmp_gen(g, 1), take(vp, 4))
        interleave(vp)
        S.barrier()
        AR.release()

        wq = AR.bf16(8 * 256).rearrange("p (k n) -> p k n", k=8)
        wk2 = AR.bf16(8 * 128).rearrange("p (k n) -> p k n", k=8)
        qf = [AR.f32(512) for _ in range(3)]
        qs = [AR.bf16(512) for _ in range(3)]
        qr = [AR.f32(512) for _ in range(3)]
        NPR = 6
        Pc = [AR.bf16(512) for _ in range(2)]
        Pr = [AR.bf16(512) for _ in range(NPR)]
        zero_b = AR.bf16(512)
        cmask4 = [AR.bf16(512) for _ in range(2)]
        mc4 = [AR.bf16(512) for _ in range(2)]
        oacc = [AR.f32(256).rearrange("p (r n) -> p r n", r=4) for _ in range(2)]
        otmp = AR.f32(256).rearrange("p (r n) -> p r n", r=4)
        obf = [AR.bf16(256) for _ in range(2)]
        rDc = [AR.f32(4) for _ in range(2)]
        ccc = [AR.f32(4) for _ in range(2)]
        rD2 = [AR.f32(4) for _ in range(2)]
        cc2 = [AR.f32(4) for _ in range(2)]
        imp4 = AR.f32(128).rearrange("p (r j) -> p r j", r=4)
        imp = [AR.f32(32) for _ in range(2)]
        m8 = [AR.f32(8) for _ in range(2)]
        selb = [AR.bf16(32) for _ in range(2)]
        self_ = [AR.f32(32) for _ in range(2)]
        LN8 = float(np.log(0.125))
        pring = [0]
        scnt = [0]
        S.op("pool", lambda e: e.memset(zero_b, 0.0), writes=["zero_b"])
        for i_, m_ in enumerate([causal, anti]):
            cp("dve", cmask4[i_].rearrange("p (r n) -> p r n", r=4), m_.unsqueeze(1).broadcast_to([128, 4, 128]),
               ["cmask"], ["cmask4"])

        def norm_rows_gen(src_ps, bk, gcol, dst, dkey, extra_bias, sidx, width=512):
            b = sidx
            ob, obk = banks[5 + sidx], "bank%d" % (5 + sidx)
            cp("act", qf[b][0:64, 0:width], src_ps, [bk], ["qf%d" % b])
            tt("dve", qs[b][0:64, 0:width], qf[b][0:64, 0:width], qf[b][0:64, 0:width], ALU.mult, ["qf%d" % b], ["qs%d" % b])
            yield
            mm(ob[0:64, 0:width], onesb2[0:64, 0:64], qs[b][0:64, 0:width], ["onesb2", "qs%d" % b], [obk])
            act(qr[b][0:64, 0:width], ob[0:64, 0:width], AF.Ln, [obk], ["qr%d" % b], scale=1.0 / 64, bias=EPS)
            yield
            act(qr[b][0:64, 0:width], qr[b][0:64, 0:width], AF.Exp, ["qr%d" % b], ["qr%d" % b], scale=-0.5, bias=extra_bias)
            vw_ = (lambda a: a.rearrange("p (a n) -> p a n", a=4)) if len(dst.shape) == 3 else (lambda a: a)
            stt(dst, vw_(qf[b][0:64, 0:width]), nsm[0:64, gcol:gcol + 1], vw_(qr[b][0:64, 0:width]), ALU.mult, ALU.mult,
                ["qf%d" % b, "nsm", "qr%d" % b], dkey)
            yield

        def mmx(out, lhsT, rhs, reads, writes, start, stop):
            S.op("pe", lambda e: e.matmul(out, lhsT=lhsT, rhs=rhs, start=start, stop=stop, skip_group_check=True),
                 reads=reads, writes=writes)

        def attend_gen(g, qt, Ktile, kkey, Vt, kts, pv, pvkey):
            qsl = Qg[0:105, qt * 512:(qt + 1) * 512]
            qreads = [kkey, "Qg", ("Qg_sel", qt), "Qg_sel_all", "Qg_c"]
            mmx(pv[:, 0:260], zero_b[:, 0:128], zero_b[:, 0:260], ["zero_b"], [pvkey], True, True)
            n = len(kts)
            pis = [None] * n

            def score(ii):
                kt, msk = kts[ii]
                bi = (1, 2, 5, 6)[scnt[0] % 4]
                scnt[0] += 1
                pb, bk = banks[bi], "bank%d" % bi
                mm(pb[:], Ktile[0:105, kt * 128:(kt + 1) * 128], qsl, qreads, [bk], start=True, stop=(msk is None))
                if msk is not None:
                    mm(pb[:], ident_b, msk, ["ident_b", "cmask4"], [bk], start=False, stop=True)
                pi = pring[0] % NPR
                pring[0] += 1
                act(Pr[pi], pb[:], AF.Exp, [bk], ["Pr%d" % pi])
                pis[ii] = pi

            score(0)
            if n > 1:
                score(1)
            for ii in range(n):
                if ii + 2 < n:
                    score(ii + 2)
                kt = kts[ii][0]
                for r in range(4):
                    mmx(pv[:, r * 65:(r + 1) * 65], Pr[pis[ii]][:, r * 128:(r + 1) * 128], Vt[:, kt, g, 0:65],
                        ["Pr%d" % pis[ii]], [pvkey], False, True)
                yield

        def finish_gen(g, qt, x, pv4, pvkey, rD_, cc_, first, last):
            p = qt % 2
            ts("dve", rD_.unsqueeze(2), pv4[:, :, 64:65], 1e-30, ALU.add, [pvkey], ["rD%d%d" % (x, p)])
            S.op("dve", lambda e: e.reciprocal(out=rD_, in_=rD_), reads=["rD%d%d" % (x, p)], writes=["rD%d%d" % (x, p)])
            tt("dve", cc_, rD_, gates[:, qt, g, :, x], ALU.mult, ["rD%d%d" % (x, p)], ["cc%d%d" % (x, p)])
            ccb = cc_.unsqueeze(2).broadcast_to([128, 4, 64])
            if first:
                tt("dve", oacc[p], pv4[:, :, 0:64], ccb, ALU.mult, [pvkey, "cc%d%d" % (x, p)], ["oacc%d" % p])
            else:
                tt("dve", otmp, pv4[:, :, 0:64], ccb, ALU.mult, [pvkey, "cc%d%d" % (x, p)], ["otmp"])
                dst = obf[p].rearrange("p (r n) -> p r n", r=4) if last else oacc[p]
                tt("dve", dst, otmp, oacc[p], ALU.add, ["otmp", "oacc%d" % p], ["obf%d" % p if last else "oacc%d" % p])
            yield

        def pass1(g, qt, pad=0):
            p = qt % 2
            qsl = Qg[0:105, qt * 512:(qt + 1) * 512]
            cp("dve", mc4[p][0:127].rearrange("p (r n) -> p r n", r=4),
               maskc[0:127, qt * 128:(qt + 1) * 128].unsqueeze(1).broadcast_to([127, 4, 128]), ["maskc"], ["mc4%d" % p])
            mm(banks[0][0:127, :], KC[g][0:105, 0:127], qsl, [("KC", g), ("KCc", g), "Qg", ("Qg_sel", qt), "Qg_sel_all", "Qg_c"],
               ["bank0"], start=True, stop=False)
            mm(banks[0][0:127, :], ident_b[0:127, 0:127], mc4[p][0:127], ["ident_b", "mc4%d" % p], ["bank0"], start=False, stop=True)
            act(Pc[p][0:127], banks[0][0:127, :], AF.Exp, ["bank0"], ["Pc%d" % p])
            yield
            b7 = banks[7][:, 0:388].rearrange("p (r n) -> p r n", r=4)
            for r in range(4):
                mm(b7[:, r, :], Pc[p][0:127, r * 128:(r + 1) * 128], VC[g][0:127, 0:97], ["Pc%d" % p, ("VC", g), ("VCc", g)], ["bank7"])
            yield
            yield from finish_gen(g, qt, 0, b7, "bank7", rDc[p], ccc[p], True, False)
            tt("dve", imp4, b7[:, :, 65:97], rDc[p].unsqueeze(2).broadcast_to([128, 4, 32]), ALU.mult, ["bank7", "rD0%d" % p], ["imp4"])
            tt("dve", imp[p], imp4[:, 0, :], imp4[:, 1, :], ALU.add, ["imp4"], ["imp%d" % p])
            tt("dve", imp[p], imp[p], imp4[:, 2, :], ALU.add, ["imp4", "imp%d" % p], ["imp%d" % p])
            tt("dve", imp[p], imp[p], imp4[:, 3, :], ALU.add, ["imp4", "imp%d" % p], ["imp%d" % p])
            yield
            tt("dve", imp[p], imp[p], bbias[:, qt, :], ALU.add, ["imp%d" % p, "bbias"], ["imp%d" % p])
            S.op("dve", lambda e: e.max(out=m8[p], in_=imp[p]), reads=["imp%d" % p], writes=["m8%d" % p])
            ts("dve", self_[p], imp[p], m8[p][:, 7:8], ALU.is_ge, ["imp%d" % p, "m8%d" % p], ["self%d" % p])
            ts("dve", selb[p], self_[p], 1.0, ALU.subtract, ["self%d" % p], ["selb%d" % p], s2=30000.0, op1=ALU.mult)
            yield
            for _ in range(pad):
                yield
            tp(bankbf[7][0:32, 800:928], selb[p], ident_b, ["selb%d" % p, "ident_b"], ["bank7"])
            cp("act", Qg4[64:96, qt, :, :], bankbf[7][0:32, 800:928].unsqueeze(1).broadcast_to([32, 4, 128]),
               ["bank7"], [("Qg_sel", qt)])
            yield

        def pass2(g, qt):
            p = qt % 2
            pvs, pvw = banks[3], banks[4]
            pv4s = pvs[:, 0:260].rearrange("p (r n) -> p r n", r=4)
            pv4w = pvw[:, 0:260].rearrange("p (r n) -> p r n", r=4)
            yield from attend_gen(g, qt, KS, "KS", VS, [(kt, cmask4[0] if kt == qt else None) for kt in range(qt + 1)],
                                  pvs, "bank3")
            kts = []
            if qt >= 2:
                kts.append((qt - 2, cmask4[1]))
            if qt >= 1:
                kts.append((qt - 1, None))
            kts.append((qt, cmask4[0]))
            ga = attend_gen(g, qt, KW, "KW", VW, kts, pvw, "bank4")
            next(ga)
            yield from finish_gen(g, qt, 1, pv4s, "bank3", rD2[0], cc2[0], False, False)
            yield from ga
            yield from finish_gen(g, qt, 2, pv4w, "bank4", rD2[1], cc2[1], False, True)

        def tail2(g, qt):
            p = qt % 2
            yield
            ob = bankbf[4]
            for c2 in range(2):
                tp(ob[:, 768 + c2 * 128:768 + (c2 + 1) * 128], obf[p][:, c2 * 128:(c2 + 1) * 128], ident_b,
                   ["obf%d" % p, "ident_b"], ["bank4"])
            cp("act", o_nsT[:, 2 * g:2 * g + 2, qt * 128:(qt + 1) * 128],
               ob[:, 768:1024].rearrange("p (c n) -> p c n", c=2), ["bank4"], [("o_nsT", g)])
            yield

        for g in range(4):
            S.dma("pool", wq, kchunk(w_in_d, C_NQ + g * 256, C_NQ + (g + 1) * 256), writes=["wq"])
            S.dma("pool", wk2[:, :, 0:64], kchunk(w_in_d, C_KS + g * 64, C_KS + (g + 1) * 64), writes=["wk2"])
            S.dma("pool", wk2[:, :, 64:128], kchunk(w_in_d, C_KW + g * 64, C_KW + (g + 1) * 64), writes=["wk2"])
            S.dma("pool", Qg[96:105, :], qrow_d[g], writes=["Qg_c"])
            tasks = []
            for r in range(4):
                for tc in range(4):
                    def task(sidx, r=r, tc=tc):
                        pb, bk = banks[sidx], "bank%d" % sidx
                        for k in range(8):
                            mm(pb[0:64, :], wq[:, k, r * 64:(r + 1) * 64], hT[:, k, tc * 512:(tc + 1) * 512], ["wq"], [bk],
                               start=(k == 0), stop=(k == 7))
                        yield
                        yield from norm_rows_gen(pb[0:64, :], bk, 0, Qg4[0:64, tc * 4:(tc + 1) * 4, r, :], ["Qg"], LN8, sidx)
                    tasks.append(task)
            for i, (Kt, kk, gcol) in enumerate([(KS, "KS", 2), (KW, "KW", 3)]):
                for tc in range(4):
                    def task(sidx, i=i, Kt=Kt, kk=kk, gcol=gcol, tc=tc):
                        pb, bk = banks[sidx], "bank%d" % sidx
                        for k in range(8):
                            mm(pb[0:64, :], wk2[:, k, i * 64:(i + 1) * 64], hT[:, k, tc * 512:(tc + 1) * 512], ["wk2"], [bk],
                               start=(k == 0), stop=(k == 7))
                        yield
                        yield from norm_rows_gen(pb[0:64, :], bk, gcol, Kt[0:64, tc * 512:(tc + 1) * 512], [kk], 0.0, sidx)
                    tasks.append(task)

            def stream_gen(sidx):
                for tk in tasks[sidx::3]:
                    yield from tk(sidx)
            interleave(stream_gen(0), stream_gen(1), stream_gen(2))
            interleave(pass1(g, 0))
            prev_tail = None
            for qt in range(NT):
                interleave(pass2(g, qt), pass1(g, qt + 1, max(0, qt - 1)) if qt + 1 < NT else None, prev_tail)
                prev_tail = tail2(g, qt)
            interleave(prev_tail)
        S.barrier()
        AR.release()
        if dbg == "nsa":
            dsc = AR.f32(T)
            for k in range(8):
                cp("dve", dsc, o_nsT[:, k, :], [], ["dsc"])
                S.dma("sp", dbg_d[:, k * T:(k + 1) * T], dsc, reads=["dsc"], writes=[("dbg", k)])
            S.finish([("dbg", k) for k in range(8)])
            S.run()
            return nc

        wout = AR.bf16(8 * D).rearrange("p (k n) -> p k n", k=8)
        AR.mark()
        wdn = AR.bf16(8 * D).rearrange("p (k n) -> p k n", k=8)
        wns = AR.bf16(8 * D).rearrange("p (k n) -> p k n", k=8)
        S.dma("pool", wdn, kchunk(w_dn_d, 0, D), writes=["wdn"])
        S.dma("pool", wns, kchunk(w_nsa_d, 0, D), writes=["wns"])
        gwf = [AR.bf16(8 * D).rearrange("p (k n) -> p k n", k=8) for _ in range(2)]
        for br_, c0_ in enumerate([C_GDN, C_GNS]):
            for q4 in range(4):
                S.dma("pool", gwf[br_][:, :, q4 * 256:(q4 + 1) * 256], kchunk(w_in_d, c0_ + q4 * 256, c0_ + (q4 + 1) * 256),
                      writes=[("gwf", br_, q4)])
        S.dma("pool", wout, kchunk(w_out_d, 0, D), writes=["wout_pre"])
        mixtmp = AR.bf16(8 * 512).rearrange("p (k n) -> p k n", k=8)
        sg = [AR.f32(512) for _ in range(2)]
        t1 = [AR.f32(512) for _ in range(2)]
        it = 0
        for tc in range(4):
            ts = slice(tc * 512, (tc + 1) * 512)
            for m in range(8):
                b = it % 2
                it += 1
                for br, (wmat, wkey, oT, okey, gkey) in enumerate(
                        [(wdn, "wdn", o_dnT, "o_dnT", ("gwf", 0, m // 2)), (wns, "wns", o_nsT, "o_nsT", ("gwf", 1, m // 2))]):
                    py, pg = banks[0 + 2 * br], banks[1 + 2 * br]
                    for k in range(8):
                        mm(py[:], wmat[:, k, m * 128:(m + 1) * 128], oT[:, k, ts], [wkey, okey, (okey, tc)],
                           ["bank%d" % (2 * br)], start=(k == 0), stop=(k == 7))
                    for k in range(8):
                        mm(pg[:], gwf[br][:, k, m * 128:(m + 1) * 128], hT[:, k, ts], [gkey, ("hT", tc)],
                           ["bank%d" % (2 * br + 1)], start=(k == 0), stop=(k == 7))
                    S.op("act", lambda e, pg=pg, br=br: e.activation(out=sg[br], in_=pg[:], func=AF.Sigmoid),
                         reads=["bank%d" % (2 * br + 1)], writes=["sg%d" % br])
                    S.op("dve", lambda e, py=py, br=br: e.tensor_tensor(out=t1[br], in0=py[:], in1=sg[br], op=ALU.mult),
                         reads=["bank%d" % (2 * br), "sg%d" % br], writes=["t1_%d" % br])
                S.op("pool", lambda e, m=m: e.tensor_tensor(out=mixtmp[:, m, :], in0=t1[0], in1=t1[1], op=ALU.add),
                     reads=["t1_0", "t1_1"], writes=[("mixtmp", m)])
            S.op("pool", lambda e, ts=ts: e.tensor_copy(out=o_dnT[:, :, ts], in_=mixtmp),
                 reads=[("mixtmp", m) for m in range(8)], writes=[("o_dnT", tc)])
        S.barrier()
        AR.release()
        mixT = o_dnT
        ARw = Arena(A_t, dead1)
        ARw.top = dead0 + 8192
        wup = [ARw.bf16(8 * 512).rearrange("p (k n) -> p k n", k=8) for _ in range(2)]
        wdw = [ARw.bf16(4 * D).rearrange("p (k n) -> p k n", k=4) for _ in range(2)]

        def ffn_load(e8):
            b_ = e8 % 2
            S.dma("pool", wup[b_], kchunk(w_up_d, e8 * 512, (e8 + 1) * 512), writes=["wup%d" % b_])
            S.dma("pool", wdw[b_], w_down_d[e8 * 512:(e8 + 1) * 512, :].rearrange("(k p) n -> p k n", p=128),
                  writes=["wdw%d" % b_])
        if dbg == "mix":
            dsc = AR.f32(T)
            for k in range(8):
                cp("dve", dsc, o_dnT[:, k, :], [], ["dsc"])
                S.dma("sp", dbg_d[:, k * T:(k + 1) * T], dsc, reads=["dsc"], writes=[("dbg", k)])
            S.finish([("dbg", k) for k in range(8)])
            S.run()
            return nc

        x1 = AR.f32(NT * D).rearrange("p (t n) -> p t n", t=NT)
        AR.mark()
        for t in range(NT):
            S.dma("sp", x1[:, t, :], x_d[t * 128:(t + 1) * 128, :], writes=[("x1", t)])
        scr = [(AR.bf16(D), AR.f32(1), AR.f32(1), AR.bf16(D)) for _ in range(2)]
        ffn_load(0)
        ffn_load(1)

        def x1_mm(t):
            for mh in range(2):
                pb = banks[mh]
                for k in range(8):
                    mm(pb[:], mixT[:, k, t * 128:(t + 1) * 128], wout[:, k, mh * 512:(mh + 1) * 512], ["wout"], ["bank%d" % mh],
                       start=(k == 0), stop=(k == 7))
                tt("dve", x1[:, t, mh * 512:(mh + 1) * 512], pb[:], x1[:, t, mh * 512:(mh + 1) * 512], ALU.add,
                   ["bank%d" % mh, ("x1", t)], [("x1", t)])
            yield

        def x1_norm(t):
            rmsnorm_to_T(x1[:, t, :], ("x1", t), gmlpB, "gmlpB", t, scr[t % 2], hT,
                         lambda t: ("hT", t // 4), "p3_%d" % (t % 2))
            yield

        interleave(x1_mm(0))
        for t in range(NT):
            interleave(x1_mm(t + 1) if t + 1 < NT else None, x1_norm(t))
        S.barrier()
        AR.release()
        h2T = hT

        AR_main = AR
        AR = Arena(A_t, dead0 + 8192)
        AR.top = dead0
        AR.mark()
        aT = [AR.bf16(4 * 512).rearrange("p (k n) -> p k n", k=4) for _ in range(2)]
        rl = [AR.f32(512) for _ in range(2)]
        ia = 0
        ir = 0
        for e8 in range(8):
            b = e8 % 2
            if e8 >= 1 and e8 + 1 < 8:
                ffn_load(e8 + 1)
            for tc in range(4):
                ts = slice(tc * 512, (tc + 1) * 512)
                ab = ia % 2
                ia += 1
                for fc in range(4):
                    pb = banks[fc % 2]
                    for k in range(8):
                        S.op("pe", lambda e, k=k, pb=pb, fc=fc, b=b, ts=ts: e.matmul(
                            pb[:], lhsT=wup[b][:, k, fc * 128:(fc + 1) * 128], rhs=h2T[:, k, ts],
                            start=(k == 0), stop=(k == 7)),
                            reads=["wup%d" % b, ("hT", tc)], writes=["bank%d" % (fc % 2)])
                    rb = ir % 2
                    ir += 1
                    S.op("act", lambda e, pb=pb, rb=rb: e.activation(out=rl[rb], in_=pb[:], func=AF.Relu),
                         reads=["bank%d" % (fc % 2)], writes=["rl%d" % rb])
                    S.op("pool", lambda e, rb=rb, ab=ab, fc=fc: e.tensor_tensor(
                        out=aT[ab][:, fc, :], in0=rl[rb], in1=rl[rb], op=ALU.mult),
                        reads=["rl%d" % rb], writes=[("aT%d" % ab, fc)])
                for tl in range(4):
                    t = tc * 4 + tl
                    for mh in range(2):
                        pb = banks[2 + mh]
                        for fc in range(4):
                            S.op("pe", lambda e, fc=fc, pb=pb, ab=ab, tl=tl, mh=mh, b=b: e.matmul(
                                pb[:], lhsT=aT[ab][:, fc, tl * 128:(tl + 1) * 128],
                                rhs=wdw[b][:, fc, mh * 512:(mh + 1) * 512],
                                start=(fc == 0), stop=(fc == 3)),
                                reads=[("aT%d" % ab, fc), "wdw%d" % b], writes=["bank%d" % (2 + mh)])
                        S.op("dve", lambda e, pb=pb, mh=mh, t=t: e.tensor_tensor(
                            out=x1[:, t, mh * 512:(mh + 1) * 512], in0=pb[:], in1=x1[:, t, mh * 512:(mh + 1) * 512],
                            op=ALU.add),
                            reads=["bank%d" % (2 + mh), ("x1", t)], writes=[("x1", t)])
                    if e8 == 7:
                        S.dma("sp", out_d[t * 128:(t + 1) * 128, :], x1[:, t, :], reads=[("x1", t)],
                              writes=[("out", t)])
        S.finish([("out", t) for t in range(NT)])
        AR.release()
        S.run()
        print("ops recorded:", S.nops, {k: len(v) for k, v in S.ops.items()})
    return nc


_NC = None
DBG = None


def _bf16_round(v):
    a = np.asarray(v, np.float32).view(np.uint32).astype(np.uint64)
    a = (a + 0x7FFF + ((a >> 16) & 1)) & 0xFFFF0000
    return a.astype(np.uint32).view(np.float32).astype(np.float64)


def _split3(v):
    v = np.asarray(v, np.float64)
    h = _bf16_round(v)
    l = _bf16_round(v - h)
    l2 = _bf16_round(v - h - l)
    return h, l, l2


def _nsa_consts():
    c = {}
    slopes = (2.0 ** (-8.0 * np.arange(1, 17) / 16)).astype(np.float32).astype(np.float64)
    qi = np.arange(128)
    qrow = np.zeros((4, 9, 16, 4, 128), np.float64)
    for g in range(4):
        for r in range(4):
            sl = slopes[4 * g + r]
            qpos = (np.arange(16)[:, None] * 128 + qi[None, :]).astype(np.float64)
            qrow[g, 0:3, :, r, :] = np.stack(_split3(-sl * qpos))
            qrow[g, 3:6, :, r, :] = np.stack(_split3(np.full((16, 128), sl)))
            qrow[g, 6:9, :, r, :] = np.stack(_split3(np.full((16, 128), 128.0 * sl)))
    c["qrow"] = qrow.reshape(4, 9, 8192).astype(np.float32)
    kpos = np.concatenate([np.arange(T), 16 * np.arange(127) + 31]).astype(np.float64)
    krow = np.zeros((9, T + 127), np.float64)
    krow[0:3] = 1.0
    krow[3:6] = kpos % 128
    krow[6:9] = kpos // 128
    c["krow"] = krow.astype(np.float32)
    c["e2"] = (np.arange(T)[None, :] // 64 == np.arange(32)[:, None]).astype(np.float32)
    n = np.arange(127)
    q = np.arange(T)
    c["maskc"] = np.where(q[None, :] - (16 * n[:, None] + 31) >= 0, 0.0, -30000.0).astype(np.float32)
    blk = q // 64
    j = np.arange(32)
    bb = np.where(j[None, :] > blk[:, None], -1e30, 0.0)
    bb = np.where((j[None, :] == blk[:, None]) | (j[None, :] == 0), 1e9, bb)
    c["bbias"] = np.ascontiguousarray(bb.reshape(16, 128, 32).transpose(1, 0, 2).reshape(128, 512)).astype(np.float32)
    ki = np.arange(128)
    causal = np.where(qi[None, :] >= ki[:, None], 0.0, -30000.0)
    anti = np.where(qi[None, :] < ki[:, None], 0.0, -30000.0)
    c["cmask"] = np.concatenate([causal, anti], axis=1).astype(np.float32)
    c0 = np.arange(127)[:, None] * 16
    j0 = np.arange(32)[None, :] * 64
    c["ov"] = (np.clip(np.minimum(c0 + 32, j0 + 64) - np.maximum(c0, j0), 0, None) / 32).astype(np.float32)
    return c


def kernel(**inputs):
    global _NC
    f = lambda a: np.ascontiguousarray(np.asarray(a, dtype=np.float32))
    x = f(inputs["x"])
    common = {
        "gmixB": f(np.broadcast_to(inputs["norm_mix"][0][None, :], (128, D))),
        "gmlpB": f(np.broadcast_to(inputs["norm_mlp"][0][None, :], (128, D))),
        "w_in": f(inputs["w_in"][0]),
        "w_dn": f(inputs["w_proj_dn"][0]),
        "w_nsa": f(inputs["w_proj_nsa"][0]),
        "w_out": f(inputs["w_out"][0]),
        "w_up": f(inputs["w_up"][0]),
        "w_down": f(inputs["w_down"][0]),
        "ident": np.eye(128, dtype=np.float32),
    }
    ii = np.arange(128)
    ones = np.ones((128, 128), np.float32)
    ltri = (ii[:, None] <= ii[None, :]).astype(np.float32)
    maskA = np.where(ii[:, None] > ii[None, :], 0.0, 1e4).astype(np.float32)
    maskQ = np.where(ii[None, :] >= ii[:, None], 0.0, -1e4).astype(np.float32)
    common["cst"] = f(np.concatenate([ones, ltri, maskA, maskQ, np.eye(128, dtype=np.float32)], axis=1))
    cw = np.asarray(inputs["dn_conv"][0], np.float32)
    common["convw"] = f(cw.reshape(4, 24, 128).transpose(2, 1, 0).reshape(128, 96))
    sm = np.zeros((128, 384), np.float32)
    sm[:, 0:128] = np.tile(np.asarray(inputs["dn_a_log"][0], np.float32), 16)[None, :]
    sm[:, 128:256] = np.tile(np.asarray(inputs["dn_dt_bias"][0], np.float32), 16)[None, :]
    sm[:, 256] = np.asarray(inputs["dn_out_norm"][0], np.float32)
    common["gdnsm"] = sm
    common.update(_nsa_consts())
    nsm = np.zeros((128, 8), np.float32)
    for c, name in enumerate(["nsa_q_norm", "nsa_k_norm_cmp", "nsa_k_norm_slc", "nsa_k_norm_win"]):
        nsm[0:64, c] = np.asarray(inputs[name][0], np.float32)
    common["nsm"] = nsm
    common["w1k"] = f(inputs["cmp_w1_k"][0])
    common["w1v"] = f(inputs["cmp_w1_v"][0])
    common["w2k"] = f(inputs["cmp_w2_k"][0])
    common["w2v"] = f(inputs["cmp_w2_v"][0])
    common["posk"] = f(np.asarray(inputs["cmp_pos_k"][0]).T)
    common["posv"] = f(np.asarray(inputs["cmp_pos_v"][0]).T)
    if _NC is None:
        _NC = build_nc(DBG)
    in_maps = [dict(common, x=f(x[i])) for i in range(8)]
    res = run_bass_kernel_spmd(_NC, in_maps, core_ids=list(range(8)))
    if DBG:
        return res.results[0]["dbg"]
    return np.stack([r["out"] for r in res.results], axis=0).astype(np.float32)
```

```python
import numpy as np
from contextlib import ExitStack
import concourse.bass as bass
import concourse.mybir as mybir
from concourse.bass_utils import run_bass_kernel_spmd

F32 = mybir.dt.float32
BF16 = mybir.dt.bfloat16
AF = mybir.ActivationFunctionType
ALU = mybir.AluOpType

N_DMA_SEMS = 24
T = 2048
D = 1024
NT = 16
EPS = 1e-6

C_QKV, C_Z, C_B, C_A, C_NQ, C_KC, C_VC, C_KS, C_VS, C_KW, C_VW, C_G, C_GDN, C_GNS = (
    0, 3072, 4096, 4104, 4112, 5136, 5392, 5648, 5904, 6160, 6416, 6672, 6720, 7744)


class Sched:
    def __init__(self, nc, stack, same_engine_waits=True):
        self.nc = nc
        self.engs = {"pe": nc.tensor, "act": nc.scalar, "dve": nc.vector,
                     "pool": nc.gpsimd, "sp": nc.sync}
        self.sem = {k: stack.enter_context(nc.semaphore("s_" + k)) for k in self.engs}
        self.cnt = {k: 0 for k in self.engs}
        self.dsem = [stack.enter_context(nc.semaphore("d%d" % i)) for i in range(N_DMA_SEMS)]
        self.dcnt = [0] * N_DMA_SEMS
        self.dnext = 0
        self.ops = {k: [] for k in self.engs}
        self.last_w = {}
        self.readers = {}
        self.waited = {k: {} for k in self.engs}
        self.same = same_engine_waits
        self.nops = 0

    def _semof(self, tag):
        return self.dsem[tag] if isinstance(tag, int) else self.sem[tag]

    def _deps(self, eng, reads, writes):
        deps = {}

        def need(tag, v):
            if deps.get(tag, 0) < v:
                deps[tag] = v
        for k in reads:
            w = self.last_w.get(k)
            if w is not None:
                need(*w)
        for k in writes:
            w = self.last_w.get(k)
            if w is not None:
                need(*w)
            for tag, v in self.readers.get(k, {}).items():
                need(tag, v)
        out = []
        for tag, v in deps.items():
            if tag == eng and (eng in ("pe", "sp") or not self.same):
                continue
            if self.waited[eng].get(tag, 0) >= v:
                continue
            self.waited[eng][tag] = v
            out.append((tag, v))
        return out

    def _emit_waits(self, eng, deps):
        for tag, v in deps:
            sem = self._semof(tag)
            self.ops[eng].append(lambda e, sem=sem, v=v: e.wait_ge(sem, v))

    def _commit(self, tag, v, reads, writes):
        for k in reads:
            self.readers.setdefault(k, {})[tag] = v
        for k in writes:
            self.last_w[k] = (tag, v)
            self.readers[k] = {}

    @staticmethod
    def _excl(reads, writes):
        isb = lambda k: isinstance(k, str) and k.startswith("bank")
        pr = [k for k in reads if isb(k)]
        if pr:
            reads = [k for k in reads if not isb(k)]
            writes = list(writes) + pr
        return reads, writes

    def op(self, eng, fn, reads=(), writes=()):
        reads, writes = self._excl(reads, writes)
        deps = self._deps(eng, reads, writes)
        self._emit_waits(eng, deps)
        self.cnt[eng] += 1
        v = self.cnt[eng]
        sem = self.sem[eng]
        self.ops[eng].append(lambda e, fn=fn, sem=sem: fn(e).then_inc(sem, 1))
        self._commit(eng, v, reads, writes)
        self.nops += 1

    def dma(self, q, out, in_, reads=(), writes=(), **kw):
        i = self.dnext
        self.dnext = (self.dnext + 1) % N_DMA_SEMS
        deps = self._deps(q, reads, writes)
        if self.dcnt[i] > 0 and self.waited[q].get(i, 0) < self.dcnt[i]:
            self.waited[q][i] = self.dcnt[i]
            deps.append((i, self.dcnt[i]))
        self._emit_waits(q, deps)
        self.dcnt[i] += 16
        v = self.dcnt[i]
        sem = self.dsem[i]
        self.ops[q].append(lambda e, out=out, in_=in_, sem=sem, kw=kw:
                           e.dma_start(out=out, in_=in_, **kw).then_inc(sem, 16))
        self._commit(i, v, reads, writes)
        self.nops += 1

    def barrier(self):
        for eng in self.engs:
            deps = []
            for tag in self.engs:
                if tag != eng and self.cnt[tag] > self.waited[eng].get(tag, 0):
                    self.waited[eng][tag] = self.cnt[tag]
                    deps.append((tag, self.cnt[tag]))
            for i in range(N_DMA_SEMS):
                if self.dcnt[i] > self.waited[eng].get(i, 0):
                    self.waited[eng][i] = self.dcnt[i]
                    deps.append((i, self.dcnt[i]))
            self._emit_waits(eng, deps)
        self.last_w = {}
        self.readers = {}

    def finish(self, keys):
        deps = self._deps("sp", keys, ())
        self._emit_waits("sp", deps)

    def run(self):
        nc = self.nc
        with nc.Block() as block:
            @block.sync
            def _(e):
                for f in self.ops["sp"]:
                    f(e)

            @block.tensor
            def _(e):
                for f in self.ops["pe"]:
                    f(e)

            @block.scalar
            def _(e):
                for f in self.ops["act"]:
                    f(e)

            @block.vector
            def _(e):
                for f in self.ops["dve"]:
                    f(e)

            @block.gpsimd
            def _(e):
                for f in self.ops["pool"]:
                    f(e)


class Arena:
    def __init__(self, A, ncols):
        self.A, self.n, self.top, self.marks = A, ncols, 0, []

    fallback = None

    def _take(self, cols):
        cols = (cols + 7) // 8 * 8
        if self.top + cols > self.n and self.fallback is not None:
            return self.fallback._take(cols)
        off = self.top
        self.top += cols
        assert self.top <= self.n, ("SBUF arena overflow", self.top, self.n)
        return off

    def f32(self, n):
        off = self._take(n)
        return self.A[:, off:off + n]

    def bf16(self, n):
        c = (n + 1) // 2
        off = self._take(c)
        return self.A[:, off:off + c].bitcast(BF16)[:, 0:n]

    def mark(self):
        self.marks.append(self.top)

    def release(self):
        self.top = self.marks.pop()


def build_nc(dbg=None):
    nc = bass.Bass("TRN2", target_bir_lowering=False)
    din = lambda name, shape: nc.dram_tensor(name, list(shape), F32, kind="ExternalInput").ap()
    x_d = din("x", [T, D])
    gmixB_d = din("gmixB", [128, D])
    gmlpB_d = din("gmlpB", [128, D])
    w_in_d = din("w_in", [D, 8768])
    w_dn_d = din("w_dn", [D, D])
    w_nsa_d = din("w_nsa", [D, D])
    w_out_d = din("w_out", [D, D])
    w_up_d = din("w_up", [D, 4096])
    w_down_d = din("w_down", [4096, D])
    ident_d = din("ident", [128, 128])
    cst_d = din("cst", [128, 5 * 128])
    convw_d = din("convw", [128, 96])
    gdnsm_d = din("gdnsm", [128, 3 * 128])
    nsm_d = din("nsm", [128, 8])
    maskc_d = din("maskc", [127, T])
    bbias_d = din("bbias", [128, 512])
    cmask_d = din("cmask", [128, 256])
    e2_d = din("e2", [32, T])
    krow_d = din("krow", [9, T + 127])
    qrow_d = din("qrow", [4, 9, 16 * 512])
    ov_d = din("ov", [127, 32])
    w1k_d = din("w1k", [2048, 128])
    w1v_d = din("w1v", [2048, 128])
    w2k_d = din("w2k", [128, 64])
    w2v_d = din("w2v", [128, 64])
    posk_d = din("posk", [64, 32])
    posv_d = din("posv", [64, 32])
    dbg_d = None
    if dbg:
        dbg_d = nc.dram_tensor("dbg", [128, 8 * T], F32, kind="ExternalOutput").ap()
    out_d = nc.dram_tensor("out", [T, D], F32, kind="ExternalOutput").ap()

    kchunk = lambda w, c0, c1: w[:, c0:c1].rearrange("(k p) n -> p k n", p=128)

    with ExitStack() as st:
        S = Sched(nc, st)
        A_t = st.enter_context(nc.sbuf_tensor("arena", [128, 53184], F32))
        AR = Arena(A_t, 53184)
        banks = [st.enter_context(nc.psum_tensor("bank%d" % i, [128, 512], F32)) for i in range(8)]
        bankbf = [b[:].bitcast(BF16) for b in banks]

        ident_b = AR.bf16(128)
        gmixB = AR.f32(D)
        gmlpB = AR.f32(D)
        hT = AR.bf16(8 * T).rearrange("p (k t) -> p k t", k=8)
        dead0 = AR.top
        o_dnT = AR.bf16(8 * T).rearrange("p (k t) -> p k t", k=8)
        o_nsT = AR.bf16(8 * T).rearrange("p (k t) -> p k t", k=8)
        dead1 = AR.top

        S.dma("pool", ident_b, ident_d, writes=["ident_b"])

        def mm(out, lhsT, rhs, reads, writes, start=True, stop=True):
            S.op("pe", lambda e: e.matmul(out, lhsT=lhsT, rhs=rhs, start=start, stop=stop), reads=reads, writes=writes)

        def tp(out, in_, ident, reads, writes):
            S.op("pe", lambda e: e.transpose(out=out, in_=in_, identity=ident), reads=reads, writes=writes)

        def act(out, in_, func, reads, writes, **kw):
            S.op("act", lambda e: e.activation(out=out, in_=in_, func=func, **kw), reads=reads, writes=writes)

        def tt(eng, out, in0, in1, op, reads, writes):
            S.op(eng, lambda e: e.tensor_tensor(out=out, in0=in0, in1=in1, op=op), reads=reads, writes=writes)

        def ts(eng, out, in0, s1, op0, reads, writes, s2=None, op1=None):
            if op1 is None:
                S.op(eng, lambda e: e.tensor_scalar(out=out, in0=in0, scalar1=s1, scalar2=None, op0=op0),
                     reads=reads, writes=writes)
            else:
                S.op(eng, lambda e: e.tensor_scalar(out=out, in0=in0, scalar1=s1, scalar2=s2, op0=op0, op1=op1),
                     reads=reads, writes=writes)

        def stt(out, in0, scalar, in1, op0, op1, reads, writes):
            S.op("dve", lambda e: e.scalar_tensor_tensor(out=out, in0=in0, scalar=scalar, in1=in1, op0=op0, op1=op1),
                 reads=reads, writes=writes)

        def cp(eng, out, in_, reads, writes):
            if eng == "act":
                S.op("act", lambda e: e.copy(out=out, in_=in_), reads=reads, writes=writes)
            else:
                S.op(eng, lambda e: e.tensor_copy(out=out, in_=in_), reads=reads, writes=writes)
        S.dma("sp", gmixB, gmixB_d, writes=["gmixB"])
        S.dma("sp", gmlpB, gmlpB_d, writes=["gmlpB"])

        def rmsnorm_to_T(src_tile, src_key, gB, gkey, t, scr, dstT, dst_key_fn, tag):
            junk, ss, rstd, hb = scr
            S.op("act", lambda e: e.activation(out=junk, in_=src_tile, func=AF.Square, accum_out=ss),
                 reads=[src_key], writes=[tag + "junk", tag + "ss"])
            S.op("act", lambda e: e.activation(out=rstd, in_=ss, func=AF.Ln, scale=1.0 / D, bias=EPS),
                 reads=[tag + "ss"], writes=[tag + "rstd"])
            S.op("act", lambda e: e.activation(out=rstd, in_=rstd, func=AF.Exp, scale=-0.5),
                 reads=[tag + "rstd"], writes=[tag + "rstd"])
            S.op("dve", lambda e: e.scalar_tensor_tensor(out=hb, in0=src_tile, scalar=rstd, in1=gB,
                                                         op0=ALU.mult, op1=ALU.mult),
                 reads=[src_key, tag + "rstd", gkey], writes=[tag + "hb"])
            pb = bankbf[7]
            for k in range(8):
                S.op("pe", lambda e, k=k: e.transpose(out=pb[:, k * 128:(k + 1) * 128],
                                                      in_=hb[:, k * 128:(k + 1) * 128], identity=ident_b),
                     reads=[tag + "hb", "ident_b"], writes=["bank7"])
            S.op("act", lambda e: e.copy(out=dstT[:, :, t * 128:(t + 1) * 128],
                                         in_=pb.rearrange("p (k t) -> p k t", k=8)),
                 reads=["bank7"], writes=[dst_key_fn(t)])

        AR.mark()
        xt = [AR.f32(D) for _ in range(2)]
        scr = [(AR.bf16(D), AR.f32(1), AR.f32(1), AR.bf16(D)) for _ in range(2)]
        for t in range(NT):
            b = t % 2
            S.dma("sp", xt[b], x_d[t * 128:(t + 1) * 128, :], writes=["xt%d" % b])
            rmsnorm_to_T(xt[b], "xt%d" % b, gmixB, "gmixB", t, scr[b], hT,
                         lambda t: ("hT", t // 4), "p0_%d" % b)
        S.barrier()
        AR.release()

        NH = 2
        AR.mark()
        ARs = Arena(A_t, dead1)
        ARs.top = dead0 + 8192
        ARs.fallback = AR
        cst = AR.f32(640)
        ones_f, Ltri, maskA, maskQ, ident_f = [cst[:, i * 128:(i + 1) * 128] for i in range(5)]
        convw = AR.f32(96)
        gdnsm = AR.f32(384)
        alog_t, dtb_t, gdn_p = gdnsm[:, 0:128], gdnsm[:, 128:256], gdnsm[:, 256:257]
        ones_b = AR.bf16(128)
        S.dma("sp", cst, cst_d, writes=["cst"])
        S.dma("sp", convw, convw_d, writes=["convw"])
        S.dma("sp", gdnsm, gdnsm_d, writes=["gdnsm"])
        S.op("pool", lambda e: e.memset(ones_b, 1.0), writes=["ones_b"])
        beta, gg, gcs, egc, bg, eend, egl, nbeta, xa, ea = [AR.f32(128) for _ in range(10)]
        wba = AR.bf16(8 * 16).rearrange("p (k n) -> p k n", k=8)
        S.dma("pool", wba, kchunk(w_in_d, C_B, C_B + 16), writes=["wba"])
        for t in range(NT):
            for k in range(8):
                mm(banks[0][:, t * 16:(t + 1) * 16], hT[:, k, t * 128:(t + 1) * 128], wba[:, k, :],
                   ["wba"], ["bank0"], start=(k == 0), stop=(k == 7))
        ba3 = banks[0][:, 0:256].rearrange("p (t c) -> p t c", t=16)
        v3 = lambda a: a.rearrange("p (t h) -> p t h", t=16)
        act(v3(beta), ba3[:, :, 0:8], AF.Sigmoid, ["bank0"], ["beta"])
        tt("dve", v3(xa), ba3[:, :, 8:16], v3(dtb_t), ALU.add, ["bank0", "gdnsm"], ["xa"])
        act(xa, xa, AF.Exp, ["xa"], ["xa"])
        act(xa, xa, AF.Ln, ["xa"], ["xa"], bias=1.0)
        act(ea, alog_t, AF.Exp, ["gdnsm"], ["ea"])
        stt(gg, xa, -1.0, ea, ALU.mult, ALU.mult, ["xa", "ea"], ["gg"])
        mm(banks[1][:, 0:128], Ltri, gg, ["cst", "gg"], ["bank1"])
        mm(banks[1][:, 128:256], ones_f, gg, ["cst", "gg"], ["bank1"])
        cp("act", gcs, banks[1][:, 0:128], ["bank1"], ["gcs"])
        act(egc, banks[1][:, 0:128], AF.Exp, ["bank1"], ["egc"])
        tt("dve", bg, beta, egc, ALU.mult, ["beta", "egc"], ["bg"])
        tt("dve", eend, banks[1][:, 128:256], gcs, ALU.subtract, ["bank1", "gcs"], ["eend"])
        act(eend, eend, AF.Exp, ["eend"], ["eend"])
        act(egl, banks[1][:, 128:256], AF.Exp, ["bank1"], ["egl"])
        ts("dve", nbeta, beta, -1.0, ALU.mult, ["beta"], ["nbeta"])
        S.barrier()

        v16 = lambda a: a.rearrange("p (c n) -> p c n", c=16)
        pers = [[v16(AR.bf16(T)) for _ in range(5)] for _ in range(NH)]
        pre = ARs.bf16(3 * 2052).rearrange("p (j n) -> p j n", j=3)
        S.op("pool", lambda e: e.memset(pre[:, :, 0:4], 0.0), writes=["pre"])
        wqkv = ARs.bf16(8 * 384).rearrange("p (k n) -> p k n", k=8)
        wz = [ARs.bf16(8 * 128).rearrange("p (k n) -> p k n", k=8) for _ in range(NH)]
        diag = ARs.bf16(12 * 128).rearrange("p (j n) -> p j n", j=12)
        s_f = [ARs.f32(512) for _ in range(2)]
        sqb = [ARs.bf16(512) for _ in range(2)]
        r_f = [ARs.f32(512) for _ in range(2)]
        qkvT = [[ARs.bf16(512) for _ in range(3)] for _ in range(2)]
        kbg4_ = [ARs.bf16(512) for _ in range(2)]
        vb4_ = [ARs.bf16(512) for _ in range(2)]
        Lg4 = ARs.f32(512)
        tmp4 = ARs.f32(512)
        D24 = ARs.f32(512)
        D34 = ARs.f32(512)
        EG4 = ARs.f32(512)
        RLb = [AR.bf16(1024).rearrange("p (a n) -> p a n", a=2) for _ in range(2)]
        Rc = [RLb[i][:, 0, :] for i in range(2)]
        Lc = [RLb[i][:, 1, :] for i in range(2)]
        Xc_ = [[AR.bf16(512) for _ in range(2)] for _ in range(2)]
        L0k_ = [AR.bf16(512) for _ in range(2)]
        L0l_ = [AR.bf16(512) for _ in range(2)]
        Rf4 = AR.f32(512)
        Rz4 = AR.bf16(512)
        XT4 = AR.bf16(512)
        Sst = [AR.f32(128) for _ in range(NH)]
        Sbf = [AR.bf16(128) for _ in range(NH)]
        Slo = [AR.bf16(128) for _ in range(NH)]
        vnew = [AR.bf16(128) for _ in range(NH)]
        oraw = [AR.f32(512) for _ in range(NH)]
        sqo = AR.bf16(512)
        ro = AR.f32(512)
        szo = AR.f32(512)
        ono = AR.f32(512)
        trb = bankbf[7]

        def interleave(*gens):
            gens = [g_ for g_ in gens if g_ is not None]
            while gens:
                for g_ in list(gens):
                    try:
                        next(g_)
                    except StopIteration:
                        gens.remove(g_)

        def chain_gens(*gens):
            for g_ in gens:
                if g_ is not None:
                    yield from g_

        b4 = lambda a: a.rearrange("p (t n) -> p t n", t=4)

        def head_setup(h, hi):
            for j in range(3):
                S.dma("pool", wqkv[:, :, j * 128:(j + 1) * 128],
                      kchunk(w_in_d, C_QKV + j * 1024 + h * 128, C_QKV + j * 1024 + (h + 1) * 128), writes=[("wqkv", j)])
            S.dma("pool", wz[hi], kchunk(w_in_d, C_Z + h * 128, C_Z + (h + 1) * 128), writes=["wz%d" % hi])
            for j in range(3):
                for tap in range(4):
                    ci = (j * 8 + h) * 4 + tap
                    ts("dve", diag[:, j * 4 + tap, :], ident_b, convw[:, ci:ci + 1], ALU.mult,
                       ["ident_b", "convw"], [("diag", j)])
            yield

        def conv_stage(h, hi, cc):
            ts_ = slice(cc * 512, (cc + 1) * 512)
            par = cc % 2
            qT_, kT_, vT_ = qkvT[par]
            for j in range(3):
                pb = banks[0]
                for k in range(8):
                    mm(pb[:], wqkv[:, k, j * 128:(j + 1) * 128], hT[:, k, ts_], [("wqkv", j)], ["bank0"],
                       start=(k == 0), stop=(k == 7))
                cp("dve" if j != 1 else "act", pre[:, j, 4 + cc * 512:4 + (cc + 1) * 512], pb[:], ["bank0"],
                   [("pre", j)])
                yield
            for j in range(3):
                pb = banks[1]
                bk = "bank1"
                for tap in range(4):
                    mm(pb[:], diag[:, j * 4 + tap, :], pre[:, j, 1 + cc * 512 + tap:1 + cc * 512 + tap + 512],
                       [("diag", j), ("pre", j), "pre"], [bk], start=(tap == 0), stop=(tap == 3))
                if j == 2:
                    act(vT_, pb[:], AF.Silu, [bk], ["vT%d" % par])
                else:
                    act(s_f[j], pb[:], AF.Silu, [bk], ["s_f%d" % j])
                    tt("pool", sqb[j], s_f[j], s_f[j], ALU.mult, ["s_f%d" % j], ["sqb%d" % j])
                yield
            for j in range(2):
                mm(banks[1][:], ones_b, sqb[j], ["ones_b", "sqb%d" % j], ["bank1"])
                act(r_f[j], banks[1][:], AF.Ln, ["bank1"], ["r_f%d" % j], bias=EPS)
                yield
            for j in range(2):
                act(r_f[j], r_f[j], AF.Exp, ["r_f%d" % j], ["r_f%d" % j], scale=-0.5)
                if j == 0:
                    stt(qT_, s_f[0], 128 ** -0.5, r_f[0], ALU.mult, ALU.mult, ["s_f0", "r_f0"], ["qT%d" % par])
                else:
                    tt("dve", kT_, s_f[1], r_f[1], ALU.mult, ["s_f1", "r_f1"], ["kT%d" % par])
                yield

        def tile_stage(h, hi, cc):
            WT, U, Kend, Qdec, QKm = pers[hi]
            P = "h%d_" % hi
            par = cc % 2
            qT_, kT_, vT_ = qkvT[par]
            qk, kk, vk = "qT%d" % par, "kT%d" % par, "vT%d" % par
            pc = cc % 2
            kbg4, vb4, L0k, L0l, Xc = kbg4_[pc], vb4_[pc], L0k_[pc], L0l_[pc], Xc_[pc]
            kbk, vbk, l0kk, l0lk = "kbg4%d" % pc, "vb4%d" % pc, "L0k%d" % pc, "L0l%d" % pc
            c0 = cc * 4
            cs = slice(c0, c0 + 4)
            col4 = lambda a: a.rearrange("p (t h) -> p t h", t=16)[:, c0:c0 + 4, h]
            bc = lambda a: col4(a).unsqueeze(2).broadcast_to([128, 4, 128])
            bm = lambda a: a.unsqueeze(1).broadcast_to([128, 4, 128])
            tsl = lambda tl: slice(tl * 128, (tl + 1) * 128)
            for tl in range(4):
                tp(trb[:, tsl(tl)], kT_[:, tsl(tl)], ident_b, [kk, "ident_b"], ["bank7"])
            for tl in range(4):
                tp(trb[:, 512 + tl * 128:512 + (tl + 1) * 128], vT_[:, tsl(tl)], ident_b, [vk, "ident_b"], ["bank7"])
            tt("dve", Kend[:, cs, :], b4(trb[:, 0:512]), bc(eend), ALU.mult, ["bank7", "eend"], [(P + "Kend", cc)])
            tt("dve", b4(kbg4), b4(trb[:, 0:512]), bc(bg), ALU.mult, ["bank7", "bg"], [kbk])
            tt("dve", b4(vb4), b4(trb[:, 512:1024]), bc(beta), ALU.mult, ["bank7", "beta"], [vbk])
            yield
            tt("pool", b4(Lg4), bm(Ltri), bc(gg), ALU.mult, ["cst", "gg"], ["Lg4"])
            mm(banks[5][:], ones_f, Lg4, ["cst", "Lg4"], ["bank5"])
            tt("dve", b4(tmp4), b4(banks[5][:]), bc(gcs), ALU.subtract, ["bank5", "gcs"], ["tmp4"])
            act(EG4, banks[5][:], AF.Exp, ["bank5"], ["EG4"])
            yield
            for tl in range(4):
                mm(banks[6][:, tsl(tl)], kT_[:, tsl(tl)], kT_[:, tsl(tl)], [kk], ["bank6"])
            tt("dve", b4(D24), b4(tmp4), bm(maskA), ALU.max, ["tmp4", "cst"], ["D24"])
            tt("dve", b4(D34), b4(tmp4), bm(maskQ), ALU.min, ["tmp4", "cst"], ["D34"])
            act(D24, D24, AF.Exp, ["D24"], ["D24"], scale=-1.0)
            tt("pool", b4(D24), b4(D24), bc(nbeta), ALU.mult, ["D24", "nbeta"], ["D24"])
            RLk = lambda i: [("RL", i, 0), ("RL", i, 1)]
            Xk = lambda i: [("X", pc, i, 0), ("X", pc, i, 1)]
            tt("dve", tmp4, banks[6][:], D24, ALU.mult, ["bank6", "D24"], ["tmp4"])
            cp("act", Lc[0], tmp4, ["tmp4"], RLk(0))
            tt("dve", L0l, tmp4, Lc[0], ALU.subtract, ["tmp4"] + RLk(0), [l0lk])
            cp("pool", L0k, Lc[0], RLk(0), [l0kk])
            yield
            for tl in range(4):
                mm(banks[5][:, tsl(tl)], kT_[:, tsl(tl)], qT_[:, tsl(tl)], [kk, qk], ["bank5"])
            act(D34, D34, AF.Exp, ["D34"], ["D34"])
            tt("dve", QKm[:, cs, :], b4(banks[5][:]), b4(D34), ALU.mult, ["bank5", "D34"], [(P + "QKm", cc)])
            tt("pool", Qdec[:, cs, :], b4(qT_), b4(EG4), ALU.mult, [qk, "EG4"], [(P + "Qdec", cc)])
            yield
            for tl in range(4):
                tp(trb[:, tsl(tl)], Lc[0][:, tsl(tl)], ident_b, RLk(0) + ["ident_b"], ["bank7"])
            cp("act", Rc[0], trb[:, 0:512], ["bank7"], RLk(0))
            tt("pool", b4(Xc[0]), b4(Rc[0]), bm(ident_b), ALU.add, RLk(0) + ["ident_b"], Xk(0))
            yield
            cur = 0
            pairs = [(0, 5, 7, "act"), (1, 6, 4, "dve")]
            for lv in range(1, 7):
                nx = 1 - cur
                for (pi, rlb, xb, ce) in pairs:
                    for t2 in range(2):
                        tl = pi * 2 + t2
                        if lv < 6:
                            mm(banks[rlb][:, t2 * 128:(t2 + 1) * 128], Lc[cur][:, tsl(tl)], Rc[cur][:, tsl(tl)],
                               [("RL", cur, pi)], ["bank%d" % rlb])
                        mm(banks[rlb][:, 256 + t2 * 128:256 + (t2 + 1) * 128], Rc[cur][:, tsl(tl)], Lc[cur][:, tsl(tl)],
                           [("RL", cur, pi)], ["bank%d" % rlb])
                for (pi, rlb, xb, ce) in pairs:
                    psl = slice(pi * 256, (pi + 1) * 256)
                    if lv < 6:
                        cp(ce, RLb[nx][:, :, psl], banks[rlb][:].rearrange("p (a n) -> p a n", a=2), ["bank%d" % rlb],
                           [("RL", nx, pi)])
                    else:
                        cp(ce, Lc[nx][:, psl], banks[rlb][:, 256:512], ["bank%d" % rlb], [("RL", nx, pi)])
                for (pi, rlb, xb, ce) in pairs:
                    for t2 in range(2):
                        tl = pi * 2 + t2
                        mm(banks[xb][:, t2 * 128:(t2 + 1) * 128], Lc[nx][:, tsl(tl)], Xc[cur][:, tsl(tl)],
                           [("RL", nx, pi), ("X", pc, cur, pi)], ["bank%d" % xb])
                for (pi, rlb, xb, ce) in pairs:
                    psl = slice(pi * 256, (pi + 1) * 256)
                    tt("dve", Xc[nx][:, psl], banks[xb][:, 0:256], Xc[cur][:, psl], ALU.add,
                       ["bank%d" % xb, ("X", pc, cur, pi)], [("X", pc, nx, pi)])
                cur = nx
                yield

        def tile_stage_b(h, hi, cc):
            WT, U, Kend, Qdec, QKm = pers[hi]
            P = "h%d_" % hi
            pc = cc % 2
            kbg4, vb4, L0k, L0l, Xc = kbg4_[pc], vb4_[pc], L0k_[pc], L0l_[pc], Xc_[pc]
            kbk, vbk, l0kk, l0lk = "kbg4%d" % pc, "vb4%d" % pc, "L0k%d" % pc, "L0l%d" % pc
            cs = slice(cc * 4, cc * 4 + 4)
            bm = lambda a: a.unsqueeze(1).broadcast_to([128, 4, 128])
            tsl = lambda tl: slice(tl * 128, (tl + 1) * 128)
            Xk = lambda i: [("X", pc, i, 0), ("X", pc, i, 1)]
            cur = 0
            X0 = Xc[cur]
            X1 = Xc[1 - cur]
            for tl in range(4):
                mm(banks[5][:, tsl(tl)], L0k[:, tsl(tl)], X0[:, tsl(tl)], [l0kk] + Xk(cur), ["bank5"], start=True, stop=False)
                mm(banks[5][:, tsl(tl)], L0l[:, tsl(tl)], X0[:, tsl(tl)], [l0lk] + Xk(cur), ["bank5"], start=False, stop=True)
            for tl in range(4):
                tp(trb[:, tsl(tl)], X0[:, tsl(tl)], ident_b, Xk(cur) + ["ident_b"], ["bank7"])
            tt("dve", Rf4, banks[5][:], X0, ALU.subtract, ["bank5"] + Xk(cur), ["Rf4"])
            cp("act", XT4, trb[:, 0:512], ["bank7"], ["XT4"])
            tt("dve", b4(Rz4), b4(Rf4), bm(ident_b), ALU.add, ["Rf4", "ident_b"], ["Rz4"])
            yield
            for tl in range(4):
                mm(banks[6][:, tsl(tl)], XT4[:, tsl(tl)], Rz4[:, tsl(tl)], ["XT4", "Rz4"], ["bank6"])
            tt("dve", X1, banks[6][:], X0, ALU.add, ["bank6"] + Xk(cur), Xk(1 - cur))
            cur = 1 - cur
            yield
            Xf = Xc[cur]
            for tl in range(4):
                mm(banks[5][:, tsl(tl)], Xf[:, tsl(tl)], vb4[:, tsl(tl)], Xk(cur) + [vbk], ["bank5"])
            for tl in range(4):
                mm(banks[6][:, tsl(tl)], kbg4[:, tsl(tl)], Xf[:, tsl(tl)], Xk(cur) + [kbk], ["bank6"])
            cp("act", U[:, cs, :], b4(banks[5][:]), ["bank5"], [(P + "U", cc)])
            cp("dve", WT[:, cs, :], b4(banks[6][:]), ["bank6"], [(P + "WT", cc)])
            yield

        def scan_gen(h, hi, cc):
            WT, U, Kend, Qdec, QKm = pers[hi]
            P = "h%d_" % hi
            if cc == 0:
                S.op("pool", lambda e: e.memset(Sst[hi], 0.0), writes=["S%d" % hi])
                S.op("pool", lambda e: e.memset(Sbf[hi], 0.0), writes=["Sbf%d" % hi])
                S.op("pool", lambda e: e.memset(Slo[hi], 0.0), writes=["Slo%d" % hi])
            for c in range(4 * cc, 4 * cc + 4):
                col = c * 8 + h
                mm(banks[2][:, 0:128], WT[:, c, :], Sbf[hi], [(P + "WT", cc), "Sbf%d" % hi], ["bank2"], start=True, stop=False)
                mm(banks[2][:, 0:128], WT[:, c, :], Slo[hi], [(P + "WT", cc), "Slo%d" % hi], ["bank2"], start=False, stop=True)
                tt("dve", vnew[hi], U[:, c, :], banks[2][:, 0:128], ALU.subtract, [(P + "U", cc), "bank2"], ["vnew%d" % hi])
                yield
                mm(banks[3][:, 0:128], Sbf[hi], Qdec[:, c, :], ["Sbf%d" % hi, (P + "Qdec", cc)], ["bank3"], start=True, stop=False)
                mm(banks[3][:, 0:128], Slo[hi], Qdec[:, c, :], ["Slo%d" % hi, (P + "Qdec", cc)], ["bank3"], start=False, stop=False)
                mm(banks[3][:, 0:128], vnew[hi], QKm[:, c, :], ["vnew%d" % hi, (P + "QKm", cc)], ["bank3"], start=False, stop=True)
                mm(banks[2][:, 128:256], Kend[:, c, :], vnew[hi], [(P + "Kend", cc), "vnew%d" % hi], ["bank2"])
                stt(Sst[hi], Sst[hi], egl[:, col:col + 1], banks[2][:, 128:256], ALU.mult, ALU.add,
                    ["S%d" % hi, "egl", "bank2"], ["S%d" % hi])
                cp("act", Sbf[hi], Sst[hi], ["S%d" % hi], ["Sbf%d" % hi])
                tt("dve", Slo[hi], Sst[hi], Sbf[hi], ALU.subtract, ["S%d" % hi, "Sbf%d" % hi], ["Slo%d" % hi])
                cp("act", oraw[hi][:, (c % 4) * 128:(c % 4 + 1) * 128], banks[3][:, 0:128], ["bank3"], ["oraw%d" % hi])
                yield
            ts_ = slice(cc * 512, (cc + 1) * 512)
            tt("pool", sqo, oraw[hi], oraw[hi], ALU.mult, ["oraw%d" % hi], ["sqo"])
            mm(banks[3][:], ones_b, sqo, ["ones_b", "sqo"], ["bank3"])
            act(ro, banks[3][:], AF.Ln, ["bank3"], ["ro"], scale=1.0 / 128, bias=EPS)
            act(ro, ro, AF.Exp, ["ro"], ["ro"], scale=-0.5)
            yield
            for k in range(8):
                mm(banks[2][:], wz[hi][:, k, :], hT[:, k, ts_], ["wz%d" % hi], ["bank2"], start=(k == 0), stop=(k == 7))
            act(szo, banks[2][:], AF.Silu, ["bank2"], ["szo"])
            tt("dve", ono, oraw[hi], ro, ALU.mult, ["oraw%d" % hi, "ro"], ["ono"])
            stt(o_dnT[:, h, ts_], ono, gdn_p, szo, ALU.mult, ALU.mult, ["ono", "gdnsm", "szo"], [("o_dnT", h)])
            yield

        interleave(head_setup(0, 0))
        interleave(conv_stage(0, 0, 0))
        pend_b = None
        pend_scan = None
        for h in range(8):
            hi = h % NH
            for cc in range(4):
                if cc < 3:
                    other = conv_stage(h, hi, cc + 1)
                elif h + 1 < 8:
                    other = chain_gens(head_setup(h + 1, (h + 1) % NH), conv_stage(h + 1, (h + 1) % NH, 0))
                else:
                    other = None
                interleave(tile_stage(h, hi, cc), other, pend_b, pend_scan)
                pend_scan = scan_gen(*pend_b_args) if pend_b is not None else None
                pend_b = tile_stage_b(h, hi, cc)
                pend_b_args = (h, hi, cc)
        interleave(pend_b, pend_scan)
        interleave(scan_gen(*pend_b_args))
        S.barrier()
        AR.release()
        if dbg == "gdn":
            dsc = AR.f32(T)
            for k in range(8):
                cp("dve", dsc, o_dnT[:, k, :], [], ["dsc"])
                S.dma("sp", dbg_d[:, k * T:(k + 1) * T], dsc, reads=["dsc"], writes=[("dbg", k)])
            S.finish([("dbg", k) for k in range(8)])
            S.run()
            return nc

        AR.mark()
        v4 = lambda a: a.rearrange("p (q r n) -> p q r n", q=16, r=4)
        onesb2 = AR.bf16(128)
        S.op("pool", lambda e: e.memset(onesb2, 1.0), writes=["onesb2"])
        Qg = AR.bf16(16 * 512)
        Qg4 = v4(Qg)
        KS = AR.bf16(T)
        KW = AR.bf16(T)
        KC = [AR.bf16(128) for _ in range(4)]
        VC = [AR.bf16(104) for _ in range(4)]
        VS = AR.bf16(16 * 4 * 66).rearrange("p (k g n) -> p k g n", k=16, g=4)
        VW = AR.bf16(16 * 4 * 66).rearrange("p (k g n) -> p k g n", k=16, g=4)
        maskc = AR.bf16(T)
        bbias = AR.f32(512).rearrange("p (q j) -> p q j", q=16)
        cmask = AR.f32(256)
        causal, anti = cmask[:, 0:128], cmask[:, 128:256]
        gates = AR.f32(768).rearrange("p (q g r x) -> p q g r x", q=16, g=4, r=4)
        nsm = AR.f32(8)
        S.dma("sp", nsm, nsm_d, writes=["nsm"])
        S.dma("pool", maskc[0:127, :], maskc_d, writes=["maskc"])
        S.dma("sp", bbias, bbias_d.rearrange("p (q j) -> p q j", q=16), writes=["bbias"])
        S.dma("sp", cmask, cmask_d, writes=["cmask"])
        S.op("pool", lambda e: e.memset(Qg[64:96, :], 0.0), writes=["Qg_sel_all"])
        S.op("pool", lambda e: e.memset(KW[64:96, :], 0.0), writes=["KWc"])
        S.dma("pool", KS[64:96, :], e2_d, writes=["KSc"])
        S.dma("pool", KS[96:105, :], krow_d[:, 0:T], writes=["KSc"])
        S.dma("pool", KW[96:105, :], krow_d[:, 0:T], writes=["KWc"])
        for g in range(4):
            S.op("pool", lambda e, g=g: e.memset(KC[g][64:96, :], 0.0), writes=[("KCc", g)])
            S.dma("pool", KC[g][96:105, 0:127], krow_d[:, T:T + 127], writes=[("KCc", g)])
            S.dma("pool", VC[g][0:127, 65:97], ov_d, writes=[("VCc", g)])
            S.op("pool", lambda e, g=g: e.memset(VC[g][:, 64:65], 1.0), writes=[("VCc", g)])
        S.op("pool", lambda e: e.memset(VS[:, :, :, 64:65], 1.0), writes=["VSc"])
        S.op("pool", lambda e: e.memset(VW[:, :, :, 64:65], 1.0), writes=["VWc"])

        AR.mark()
        wv = AR.bf16(8 * 512).rearrange("p (k n) -> p k n", k=8)
        S.dma("pool", wv[:, :, 0:256], kchunk(w_in_d, C_VS, C_VS + 256), writes=["wv"])
        S.dma("pool", wv[:, :, 256:512], kchunk(w_in_d, C_VW, C_VW + 256), writes=["wv"])
        wg = AR.bf16(8 * 48).rearrange("p (k n) -> p k n", k=8)
        S.dma("pool", wg, kchunk(w_in_d, C_G, C_G + 48), writes=["wg"])
        def vproj_gen():
            for t in range(NT):
                tsl = slice(t * 128, (t + 1) * 128)
                for k in range(8):
                    mm(banks[2][:], hT[:, k, tsl], wv[:, k, :], ["wv"], ["bank2"], start=(k == 0), stop=(k == 7))
                pv = banks[2][:].rearrange("p (a g n) -> p a g n", a=2, g=4)
                cp("act", VS[:, t, :, 0:64], pv[:, 0], ["bank2", "VSc"], [("VS", t)])
                cp("dve", VW[:, t, :, 0:64], pv[:, 1], ["bank2", "VWc"], [("VW", t)])
                for k in range(8):
                    mm(banks[3][:, 0:48], hT[:, k, tsl], wg[:, k, :], ["wg"], ["bank3"], start=(k == 0), stop=(k == 7))
                act(gates[:, t].rearrange("p g r x -> p (g r x)"), banks[3][:, 0:48], AF.Sigmoid, ["bank3"], [("gates", t)])
                yield

        def take(gen, n):
            for _ in range(n):
                try:
                    next(gen)
                except StopIteration:
                    return
                yield

        w1 = [AR.bf16(32 * 128).rearrange("p (l j) -> p l j", l=32) for _ in range(2)]
        w2 = [AR.bf16(64) for _ in range(2)]
        posT = [AR.bf16(32) for _ in range(2)]
        bj = [AR.f32(1) for _ in range(2)]
        for i, (w1d, w2d, pd) in enumerate([(w1k_d, w2k_d, posk_d), (w1v_d, w2v_d, posv_d)]):
            S.dma("pool", w1[i][0:64], w1d.rearrange("(l d) j -> d l j", d=64), writes=[("w1", i)])
            S.dma("pool", w2[i], w2d, writes=[("w2", i)])
            S.dma("pool", posT[i][0:64], pd, writes=[("posT", i)])
            for l in range(32):
                mm(banks[4][:, 0:1], w1[i][0:64, l, :], posT[i][0:64, l:l + 1], [("w1", i), ("posT", i)], ["bank4"],
                   start=(l == 0), stop=(l == 31))
            cp("act", bj[i], banks[4][:, 0:1], ["bank4"], [("bj", i)])
        wkv = [AR.bf16(8 * 128).rearrange("p (k n) -> p k n", k=8) for _ in range(2)]
        craw = [AR.bf16(T + 16) for _ in range(2)]
        xh = [AR.f32(128) for _ in range(2)]
        x2 = [AR.f32(128) for _ in range(2)]
        sgm = [AR.f32(128) for _ in range(2)]
        gl_ = [AR.bf16(128) for _ in range(2)]
        kcf = AR.f32(128)
        kcs = AR.bf16(128)
        kcr = AR.f32(128)

        def wkv_load(g):
            S.dma("pool", wkv[g % 2][:, :, 0:64], kchunk(w_in_d, C_KC + g * 64, C_KC + (g + 1) * 64), writes=[("wkv", g % 2)])
            S.dma("pool", wkv[g % 2][:, :, 64:128], kchunk(w_in_d, C_VC + g * 64, C_VC + (g + 1) * 64), writes=[("wkv", g % 2)])

        def cmp_gen(g, i):
            pbn, hbn, obn = i, 4 + 2 * i, 5 + 2 * i
            pb, hb, ob = banks[pbn], banks[hbn], banks[obn]
            pk, hk, ok = "bank%d" % pbn, "bank%d" % hbn, "bank%d" % obn
            X, X2, SG, GL = xh[i], x2[i], sgm[i], gl_[i]
            for tc in range(4):
                for k in range(8):
                    mm(pb[0:64, :], wkv[g % 2][:, k, i * 64:(i + 1) * 64], hT[:, k, tc * 512:(tc + 1) * 512], [("wkv", g % 2)], [pk],
                       start=(k == 0), stop=(k == 7))
                cp("act" if tc % 2 else "dve", craw[i][0:64, tc * 512:(tc + 1) * 512], pb[0:64, :], [pk], [("craw", i)])
                yield
            c3 = craw[i][0:64, 0:T].rearrange("p (n s) -> p n s", s=16)
            for l in range(32):
                rhs = c3[:, 0:127, l] if l < 16 else c3[:, 1:128, l - 16]
                mm(hb[:, 0:127], w1[i][0:64, l, :], rhs, [("w1", i), ("craw", i)], [hk], start=(l == 0), stop=(l == 31))
                if l % 8 == 7:
                    yield
            act(X[:, 0:127], hb[:, 0:127], AF.Identity, [hk, ("bj", i)], ["xh%d" % i], bias=bj[i])
            tt("pool", X2[:, 0:127], X[:, 0:127], X[:, 0:127], ALU.mult, ["xh%d" % i], ["x2%d" % i])
            yield
            ts("dve", X2[:, 0:127], X2[:, 0:127], 0.044715, ALU.mult, ["x2%d" % i], ["x2%d" % i], s2=1.0, op1=ALU.add)
            tt("dve", X2[:, 0:127], X2[:, 0:127], X[:, 0:127], ALU.mult, ["x2%d" % i, "xh%d" % i], ["x2%d" % i])
            act(SG[:, 0:127], X2[:, 0:127], AF.Sigmoid, ["x2%d" % i], ["sgm%d" % i], scale=1.5957691216057308)
            tt("dve", GL[:, 0:127], X[:, 0:127], SG[:, 0:127], ALU.mult, ["xh%d" % i, "sgm%d" % i], ["gl%d" % i])
            yield
            if i == 0:
                mm(ob[0:64, 0:127], w2[0], GL[:, 0:127], [("w2", 0), "gl0"], [ok])
                cp("act", kcf[0:64, 0:127], ob[0:64, 0:127], [ok], ["kcf"])
                tt("pool", kcs[0:64, 0:127], kcf[0:64, 0:127], kcf[0:64, 0:127], ALU.mult, ["kcf"], ["kcs"])
                yield
                mm(ob[0:64, 128:255], onesb2[0:64, 0:64], kcs[0:64, 0:127], ["onesb2", "kcs"], [ok])
                act(kcr[0:64, 0:127], ob[0:64, 128:255], AF.Ln, [ok], ["kcr"], scale=1.0 / 64, bias=EPS)
                act(kcr[0:64, 0:127], kcr[0:64, 0:127], AF.Exp, ["kcr"], ["kcr"], scale=-0.5)
                stt(KC[g][0:64, 0:127], kcf[0:64, 0:127], nsm[0:64, 1:2], kcr[0:64, 0:127], ALU.mult, ALU.mult,
                    ["kcf", "nsm", "kcr"], [("KC", g)])
            else:
                mm(ob[0:127, 256:320], GL[:, 0:127], w2[1], [("w2", 1), "gl1"], [ok])
                cp("act", VC[g][0:127, 0:64], ob[0:127, 256:320], [ok], [("VC", g)])
            yield

        vp = vproj_gen()
        wkv_load(0)
        for g in range(4):
            if g + 1 < 4:
                wkv_load(g + 1)
            interleave(cmp_gen(g, 0), cmp_gen(g, 1), take(vp, 4))
        interleave(vp)
        S.barrier()
        AR.release()

        wq = AR.bf16(8 * 256).rearrange("p (k n) -> p k n", k=8)
        wk2 = AR.bf16(8 * 128).rearrange("p (k n) -> p k n", k=8)
        qf = [AR.f32(512) for _ in range(3)]
        qs = [AR.bf16(512) for _ in range(3)]
        qr = [AR.f32(512) for _ in range(3)]
        NPR = 6
        Pc = [AR.bf16(512) for _ in range(2)]
        Pr = [AR.bf16(512) for _ in range(NPR)]
        zero_b = AR.bf16(512)
        cmask4 = [AR.bf16(512) for _ in range(2)]
        mc4 = [AR.bf16(512) for _ in range(2)]
        oacc = [AR.f32(256).rearrange("p (r n) -> p r n", r=4) for _ in range(2)]
        otmp = AR.f32(256).rearrange("p (r n) -> p r n", r=4)
        obf = [AR.bf16(256) for _ in range(2)]
        rDc = [AR.f32(4) for _ in range(2)]
        ccc = [AR.f32(4) for _ in range(2)]
        rD2 = [AR.f32(4) for _ in range(2)]
        cc2 = [AR.f32(4) for _ in range(2)]
        imp4 = AR.f32(128).rearrange("p (r j) -> p r j", r=4)
        imp = [AR.f32(32) for _ in range(2)]
        m8 = [AR.f32(8) for _ in range(2)]
        selb = [AR.bf16(32) for _ in range(2)]
        self_ = [AR.f32(32) for _ in range(2)]
        LN8 = float(np.log(0.125))
        pring = [0]
        scnt = [0]
        S.op("pool", lambda e: e.memset(zero_b, 0.0), writes=["zero_b"])
        for i_, m_ in enumerate([causal, anti]):
            cp("dve", cmask4[i_].rearrange("p (r n) -> p r n", r=4), m_.unsqueeze(1).broadcast_to([128, 4, 128]),
               ["cmask"], ["cmask4"])

        def norm_rows_gen(src_ps, bk, gcol, dst, dkey, extra_bias, sidx, width=512):
            b = sidx
            ob, obk = banks[5 + sidx], "bank%d" % (5 + sidx)
            cp("act", qf[b][0:64, 0:width], src_ps, [bk], ["qf%d" % b])
            tt("dve", qs[b][0:64, 0:width], qf[b][0:64, 0:width], qf[b][0:64, 0:width], ALU.mult, ["qf%d" % b], ["qs%d" % b])
            yield
            mm(ob[0:64, 0:width], onesb2[0:64, 0:64], qs[b][0:64, 0:width], ["onesb2", "qs%d" % b], [obk])
            act(qr[b][0:64, 0:width], ob[0:64, 0:width], AF.Ln, [obk], ["qr%d" % b], scale=1.0 / 64, bias=EPS)
            yield
            act(qr[b][0:64, 0:width], qr[b][0:64, 0:width], AF.Exp, ["qr%d" % b], ["qr%d" % b], scale=-0.5, bias=extra_bias)
            vw_ = (lambda a: a.rearrange("p (a n) -> p a n", a=4)) if len(dst.shape) == 3 else (lambda a: a)
            stt(dst, vw_(qf[b][0:64, 0:width]), nsm[0:64, gcol:gcol + 1], vw_(qr[b][0:64, 0:width]), ALU.mult, ALU.mult,
                ["qf%d" % b, "nsm", "qr%d" % b], dkey)
            yield

        def mmx(out, lhsT, rhs, reads, writes, start, stop):
            S.op("pe", lambda e: e.matmul(out, lhsT=lhsT, rhs=rhs, start=start, stop=stop, skip_group_check=True),
                 reads=reads, writes=writes)

        def attend_gen(g, qt, Ktile, kkey, Vt, kts, pv, pvkey):
            qsl = Qg[0:105, qt * 512:(qt + 1) * 512]
            qreads = [kkey, "Qg", ("Qg_sel", qt), "Qg_sel_all", "Qg_c"]
            mmx(pv[:, 0:260], zero_b[:, 0:128], zero_b[:, 0:260], ["zero_b"], [pvkey], True, True)
            n = len(kts)
            pis = [None] * n

            def score(ii):
                kt, msk = kts[ii]
                bi = (1, 2, 5, 6)[scnt[0] % 4]
                scnt[0] += 1
                pb, bk = banks[bi], "bank%d" % bi
                mm(pb[:], Ktile[0:105, kt * 128:(kt + 1) * 128], qsl, qreads, [bk], start=True, stop=(msk is None))
                if msk is not None:
                    mm(pb[:], ident_b, msk, ["ident_b", "cmask4"], [bk], start=False, stop=True)
                pi = pring[0] % NPR
                pring[0] += 1
                act(Pr[pi], pb[:], AF.Exp, [bk], ["Pr%d" % pi])
                pis[ii] = pi

            score(0)
            if n > 1:
                score(1)
            for ii in range(n):
                if ii + 2 < n:
                    score(ii + 2)
                kt = kts[ii][0]
                for r in range(4):
                    mmx(pv[:, r * 65:(r + 1) * 65], Pr[pis[ii]][:, r * 128:(r + 1) * 128], Vt[:, kt, g, 0:65],
                        ["Pr%d" % pis[ii]], [pvkey], False, True)
                yield

        def finish_gen(g, qt, x, pv4, pvkey, rD_, cc_, first, last):
            p = qt % 2
            ts("dve", rD_.unsqueeze(2), pv4[:, :, 64:65], 1e-30, ALU.add, [pvkey], ["rD%d%d" % (x, p)])
            S.op("dve", lambda e: e.reciprocal(out=rD_, in_=rD_), reads=["rD%d%d" % (x, p)], writes=["rD%d%d" % (x, p)])
            tt("dve", cc_, rD_, gates[:, qt, g, :, x], ALU.mult, ["rD%d%d" % (x, p)], ["cc%d%d" % (x, p)])
            ccb = cc_.unsqueeze(2).broadcast_to([128, 4, 64])
            if first:
                tt("dve", oacc[p], pv4[:, :, 0:64], ccb, ALU.mult, [pvkey, "cc%d%d" % (x, p)], ["oacc%d" % p])
            else:
                tt("dve", otmp, pv4[:, :, 0:64], ccb, ALU.mult, [pvkey, "cc%d%d" % (x, p)], ["otmp"])
                dst = obf[p].rearrange("p (r n) -> p r n", r=4) if last else oacc[p]
                tt("dve", dst, otmp, oacc[p], ALU.add, ["otmp", "oacc%d" % p], ["obf%d" % p if last else "oacc%d" % p])
            yield

        def pass1(g, qt, pad=0):
            p = qt % 2
            qsl = Qg[0:105, qt * 512:(qt + 1) * 512]
            cp("dve", mc4[p][0:127].rearrange("p (r n) -> p r n", r=4),
               maskc[0:127, qt * 128:(qt + 1) * 128].unsqueeze(1).broadcast_to([127, 4, 128]), ["maskc"], ["mc4%d" % p])
            mm(banks[0][0:127, :], KC[g][0:105, 0:127], qsl, [("KC", g), ("KCc", g), "Qg", ("Qg_sel", qt), "Qg_sel_all", "Qg_c"],
               ["bank0"], start=True, stop=False)
            mm(banks[0][0:127, :], ident_b[0:127, 0:127], mc4[p][0:127], ["ident_b", "mc4%d" % p], ["bank0"], start=False, stop=True)
            act(Pc[p][0:127], banks[0][0:127, :], AF.Exp, ["bank0"], ["Pc%d" % p])
            yield
            b7 = banks[7][:, 0:388].rearrange("p (r n) -> p r n", r=4)
            for r in range(4):
                mm(b7[:, r, :], Pc[p][0:127, r * 128:(r + 1) * 128], VC[g][0:127, 0:97], ["Pc%d" % p, ("VC", g), ("VCc", g)], ["bank7"])
            yield
            yield from finish_gen(g, qt, 0, b7, "bank7", rDc[p], ccc[p], True, False)
            tt("dve", imp4, b7[:, :, 65:97], rDc[p].unsqueeze(2).broadcast_to([128, 4, 32]), ALU.mult, ["bank7", "rD0%d" % p], ["imp4"])
            tt("dve", imp[p], imp4[:, 0, :], imp4[:, 1, :], ALU.add, ["imp4"], ["imp%d" % p])
            tt("dve", imp[p], imp[p], imp4[:, 2, :], ALU.add, ["imp4", "imp%d" % p], ["imp%d" % p])
            tt("dve", imp[p], imp[p], imp4[:, 3, :], ALU.add, ["imp4", "imp%d" % p], ["imp%d" % p])
            yield
            tt("dve", imp[p], imp[p], bbias[:, qt, :], ALU.add, ["imp%d" % p, "bbias"], ["imp%d" % p])
            S.op("dve", lambda e: e.max(out=m8[p], in_=imp[p]), reads=["imp%d" % p], writes=["m8%d" % p])
            ts("dve", self_[p], imp[p], m8[p][:, 7:8], ALU.is_ge, ["imp%d" % p, "m8%d" % p], ["self%d" % p])
            ts("dve", selb[p], self_[p], 1.0, ALU.subtract, ["self%d" % p], ["selb%d" % p], s2=30000.0, op1=ALU.mult)
            yield
            for _ in range(pad):
                yield
            tp(bankbf[7][0:32, 800:928], selb[p], ident_b, ["selb%d" % p, "ident_b"], ["bank7"])
            cp("act", Qg4[64:96, qt, :, :], bankbf[7][0:32, 800:928].unsqueeze(1).broadcast_to([32, 4, 128]),
               ["bank7"], [("Qg_sel", qt)])
            yield

        def pass2(g, qt):
            p = qt % 2
            pvs, pvw = banks[3], banks[4]
            pv4s = pvs[:, 0:260].rearrange("p (r n) -> p r n", r=4)
            pv4w = pvw[:, 0:260].rearrange("p (r n) -> p r n", r=4)
            yield from attend_gen(g, qt, KS, "KS", VS, [(kt, cmask4[0] if kt == qt else None) for kt in range(qt + 1)],
                                  pvs, "bank3")
            kts = []
            if qt >= 2:
                kts.append((qt - 2, cmask4[1]))
            if qt >= 1:
                kts.append((qt - 1, None))
            kts.append((qt, cmask4[0]))
            ga = attend_gen(g, qt, KW, "KW", VW, kts, pvw, "bank4")
            next(ga)
            yield from finish_gen(g, qt, 1, pv4s, "bank3", rD2[0], cc2[0], False, False)
            yield from ga
            yield from finish_gen(g, qt, 2, pv4w, "bank4", rD2[1], cc2[1], False, True)

        def tail2(g, qt):
            p = qt % 2
            yield
            ob = bankbf[4]
            for c2 in range(2):
                tp(ob[:, 768 + c2 * 128:768 + (c2 + 1) * 128], obf[p][:, c2 * 128:(c2 + 1) * 128], ident_b,
                   ["obf%d" % p, "ident_b"], ["bank4"])
            cp("act", o_nsT[:, 2 * g:2 * g + 2, qt * 128:(qt + 1) * 128],
               ob[:, 768:1024].rearrange("p (c n) -> p c n", c=2), ["bank4"], [("o_nsT", g)])
            yield

        for g in range(4):
            S.dma("pool", wq, kchunk(w_in_d, C_NQ + g * 256, C_NQ + (g + 1) * 256), writes=["wq"])
            S.dma("pool", wk2[:, :, 0:64], kchunk(w_in_d, C_KS + g * 64, C_KS + (g + 1) * 64), writes=["wk2"])
            S.dma("pool", wk2[:, :, 64:128], kchunk(w_in_d, C_KW + g * 64, C_KW + (g + 1) * 64), writes=["wk2"])
            S.dma("pool", Qg[96:105, :], qrow_d[g], writes=["Qg_c"])
            tasks = []
            for r in range(4):
                for tc in range(4):
                    def task(sidx, r=r, tc=tc):
                        pb, bk = banks[sidx], "bank%d" % sidx
                        for k in range(8):
                            mm(pb[0:64, :], wq[:, k, r * 64:(r + 1) * 64], hT[:, k, tc * 512:(tc + 1) * 512], ["wq"], [bk],
                               start=(k == 0), stop=(k == 7))
                        yield
                        yield from norm_rows_gen(pb[0:64, :], bk, 0, Qg4[0:64, tc * 4:(tc + 1) * 4, r, :], ["Qg"], LN8, sidx)
                    tasks.append(task)
            for i, (Kt, kk, gcol) in enumerate([(KS, "KS", 2), (KW, "KW", 3)]):
                for tc in range(4):
                    def task(sidx, i=i, Kt=Kt, kk=kk, gcol=gcol, tc=tc):
                        pb, bk = banks[sidx], "bank%d" % sidx
                        for k in range(8):
                            mm(pb[0:64, :], wk2[:, k, i * 64:(i + 1) * 64], hT[:, k, tc * 512:(tc + 1) * 512], ["wk2"], [bk],
                               start=(k == 0), stop=(k == 7))
                        yield
                        yield from norm_rows_gen(pb[0:64, :], bk, gcol, Kt[0:64, tc * 512:(tc + 1) * 512], [kk], 0.0, sidx)
                    tasks.append(task)

            def stream_gen(sidx):
                for tk in tasks[sidx::3]:
                    yield from tk(sidx)
            interleave(stream_gen(0), stream_gen(1), stream_gen(2))
            interleave(pass1(g, 0))
            prev_tail = None
            for qt in range(NT):
                interleave(pass2(g, qt), pass1(g, qt + 1, max(0, qt - 1)) if qt + 1 < NT else None, prev_tail)
                prev_tail = tail2(g, qt)
            interleave(prev_tail)
        S.barrier()
        AR.release()
        if dbg == "nsa":
            dsc = AR.f32(T)
            for k in range(8):
                cp("dve", dsc, o_nsT[:, k, :], [], ["dsc"])
                S.dma("sp", dbg_d[:, k * T:(k + 1) * T], dsc, reads=["dsc"], writes=[("dbg", k)])
            S.finish([("dbg", k) for k in range(8)])
            S.run()
            return nc

        wout = AR.bf16(8 * D).rearrange("p (k n) -> p k n", k=8)
        AR.mark()
        wdn = AR.bf16(8 * D).rearrange("p (k n) -> p k n", k=8)
        wns = AR.bf16(8 * D).rearrange("p (k n) -> p k n", k=8)
        S.dma("pool", wdn, kchunk(w_dn_d, 0, D), writes=["wdn"])
        S.dma("pool", wns, kchunk(w_nsa_d, 0, D), writes=["wns"])
        gwf = [AR.bf16(8 * D).rearrange("p (k n) -> p k n", k=8) for _ in range(2)]
        for br_, c0_ in enumerate([C_GDN, C_GNS]):
            for q4 in range(4):
                S.dma("pool", gwf[br_][:, :, q4 * 256:(q4 + 1) * 256], kchunk(w_in_d, c0_ + q4 * 256, c0_ + (q4 + 1) * 256),
                      writes=[("gwf", br_, q4)])
        S.dma("pool", wout, kchunk(w_out_d, 0, D), writes=["wout_pre"])
        mixtmp = AR.bf16(8 * 512).rearrange("p (k n) -> p k n", k=8)
        sg = [AR.f32(512) for _ in range(2)]
        t1 = [AR.f32(512) for _ in range(2)]
        it = 0
        for tc in range(4):
            ts = slice(tc * 512, (tc + 1) * 512)
            for m in range(8):
                b = it % 2
                it += 1
                for br, (wmat, wkey, oT, okey, gkey) in enumerate(
                        [(wdn, "wdn", o_dnT, "o_dnT", ("gwf", 0, m // 2)), (wns, "wns", o_nsT, "o_nsT", ("gwf", 1, m // 2))]):
                    py, pg = banks[0 + 2 * br], banks[1 + 2 * br]
                    for k in range(8):
                        mm(py[:], wmat[:, k, m * 128:(m + 1) * 128], oT[:, k, ts], [wkey, okey, (okey, tc)],
                           ["bank%d" % (2 * br)], start=(k == 0), stop=(k == 7))
                    for k in range(8):
                        mm(pg[:], gwf[br][:, k, m * 128:(m + 1) * 128], hT[:, k, ts], [gkey, ("hT", tc)],
                           ["bank%d" % (2 * br + 1)], start=(k == 0), stop=(k == 7))
                    S.op("act", lambda e, pg=pg, br=br: e.activation(out=sg[br], in_=pg[:], func=AF.Sigmoid),
                         reads=["bank%d" % (2 * br + 1)], writes=["sg%d" % br])
                    S.op("dve", lambda e, py=py, br=br: e.tensor_tensor(out=t1[br], in0=py[:], in1=sg[br], op=ALU.mult),
                         reads=["bank%d" % (2 * br), "sg%d" % br], writes=["t1_%d" % br])
                S.op("pool", lambda e, m=m: e.tensor_tensor(out=mixtmp[:, m, :], in0=t1[0], in1=t1[1], op=ALU.add),
                     reads=["t1_0", "t1_1"], writes=[("mixtmp", m)])
            S.op("pool", lambda e, ts=ts: e.tensor_copy(out=o_dnT[:, :, ts], in_=mixtmp),
                 reads=[("mixtmp", m) for m in range(8)], writes=[("o_dnT", tc)])
        S.barrier()
        AR.release()
        mixT = o_dnT
        ARw = Arena(A_t, dead1)
        ARw.top = dead0 + 8192
        wup = [ARw.bf16(8 * 512).rearrange("p (k n) -> p k n", k=8) for _ in range(2)]
        wdw = [ARw.bf16(4 * D).rearrange("p (k n) -> p k n", k=4) for _ in range(2)]

        def ffn_load(e8):
            b_ = e8 % 2
            S.dma("pool", wup[b_], kchunk(w_up_d, e8 * 512, (e8 + 1) * 512), writes=["wup%d" % b_])
            S.dma("pool", wdw[b_], w_down_d[e8 * 512:(e8 + 1) * 512, :].rearrange("(k p) n -> p k n", p=128),
                  writes=["wdw%d" % b_])
        if dbg == "mix":
            dsc = AR.f32(T)
            for k in range(8):
                cp("dve", dsc, o_dnT[:, k, :], [], ["dsc"])
                S.dma("sp", dbg_d[:, k * T:(k + 1) * T], dsc, reads=["dsc"], writes=[("dbg", k)])
            S.finish([("dbg", k) for k in range(8)])
            S.run()
            return nc

        x1 = AR.f32(NT * D).rearrange("p (t n) -> p t n", t=NT)
        AR.mark()
        for t in range(NT):
            S.dma("sp", x1[:, t, :], x_d[t * 128:(t + 1) * 128, :], writes=[("x1", t)])
        scr = [(AR.bf16(D), AR.f32(1), AR.f32(1), AR.bf16(D)) for _ in range(2)]
        ffn_load(0)
        ffn_load(1)

        def x1_mm(t):
            for mh in range(2):
                pb = banks[mh]
                for k in range(8):
                    mm(pb[:], mixT[:, k, t * 128:(t + 1) * 128], wout[:, k, mh * 512:(mh + 1) * 512], ["wout"], ["bank%d" % mh],
                       start=(k == 0), stop=(k == 7))
                tt("dve", x1[:, t, mh * 512:(mh + 1) * 512], pb[:], x1[:, t, mh * 512:(mh + 1) * 512], ALU.add,
                   ["bank%d" % mh, ("x1", t)], [("x1", t)])
            yield

        def x1_norm(t):
            rmsnorm_to_T(x1[:, t, :], ("x1", t), gmlpB, "gmlpB", t, scr[t % 2], hT,
                         lambda t: ("hT", t // 4), "p3_%d" % (t % 2))
            yield

        interleave(x1_mm(0))
        for t in range(NT):
            interleave(x1_mm(t + 1) if t + 1 < NT else None, x1_norm(t))
        S.barrier()
        AR.release()
        h2T = hT

        AR_main = AR
        AR = Arena(A_t, dead0 + 8192)
        AR.top = dead0
        AR.mark()
        aT = [AR.bf16(4 * 512).rearrange("p (k n) -> p k n", k=4) for _ in range(2)]
        rl = [AR.f32(512) for _ in range(2)]
        ia = 0
        ir = 0
        for e8 in range(8):
            b = e8 % 2
            if e8 >= 1 and e8 + 1 < 8:
                ffn_load(e8 + 1)
            for tc in range(4):
                ts = slice(tc * 512, (tc + 1) * 512)
                ab = ia % 2
                ia += 1
                for fc in range(4):
                    pb = banks[fc % 2]
                    for k in range(8):
                        S.op("pe", lambda e, k=k, pb=pb, fc=fc, b=b, ts=ts: e.matmul(
                            pb[:], lhsT=wup[b][:, k, fc * 128:(fc + 1) * 128], rhs=h2T[:, k, ts],
                            start=(k == 0), stop=(k == 7)),
                            reads=["wup%d" % b, ("hT", tc)], writes=["bank%d" % (fc % 2)])
                    rb = ir % 2
                    ir += 1
                    S.op("act", lambda e, pb=pb, rb=rb: e.activation(out=rl[rb], in_=pb[:], func=AF.Relu),
                         reads=["bank%d" % (fc % 2)], writes=["rl%d" % rb])
                    S.op("pool", lambda e, rb=rb, ab=ab, fc=fc: e.tensor_tensor(
                        out=aT[ab][:, fc, :], in0=rl[rb], in1=rl[rb], op=ALU.mult),
                        reads=["rl%d" % rb], writes=[("aT%d" % ab, fc)])
                for tl in range(4):
                    t = tc * 4 + tl
                    for mh in range(2):
                        pb = banks[2 + mh]
                        for fc in range(4):
                            S.op("pe", lambda e, fc=fc, pb=pb, ab=ab, tl=tl, mh=mh, b=b: e.matmul(
                                pb[:], lhsT=aT[ab][:, fc, tl * 128:(tl + 1) * 128],
                                rhs=wdw[b][:, fc, mh * 512:(mh + 1) * 512],
                                start=(fc == 0), stop=(fc == 3)),
                                reads=[("aT%d" % ab, fc), "wdw%d" % b], writes=["bank%d" % (2 + mh)])
                        S.op("dve", lambda e, pb=pb, mh=mh, t=t: e.tensor_tensor(
                            out=x1[:, t, mh * 512:(mh + 1) * 512], in0=pb[:], in1=x1[:, t, mh * 512:(mh + 1) * 512],
                            op=ALU.add),
                            reads=["bank%d" % (2 + mh), ("x1", t)], writes=[("x1", t)])
                    if e8 == 7:
                        S.dma("sp", out_d[t * 128:(t + 1) * 128, :], x1[:, t, :], reads=[("x1", t)],
                              writes=[("out", t)])
        S.finish([("out", t) for t in range(NT)])
        AR.release()
        S.run()
        print("ops recorded:", S.nops, {k: len(v) for k, v in S.ops.items()})
    return nc


_NC = None
DBG = None


def _bf16_round(v):
    a = np.asarray(v, np.float32).view(np.uint32).astype(np.uint64)
    a = (a + 0x7FFF + ((a >> 16) & 1)) & 0xFFFF0000
    return a.astype(np.uint32).view(np.float32).astype(np.float64)


def _split3(v):
    v = np.asarray(v, np.float64)
    h = _bf16_round(v)
    l = _bf16_round(v - h)
    l2 = _bf16_round(v - h - l)
    return h, l, l2


def _nsa_consts():
    c = {}
    slopes = (2.0 ** (-8.0 * np.arange(1, 17) / 16)).astype(np.float32).astype(np.float64)
    qi = np.arange(128)
    qrow = np.zeros((4, 9, 16, 4, 128), np.float64)
    for g in range(4):
        for r in range(4):
            sl = slopes[4 * g + r]
            qpos = (np.arange(16)[:, None] * 128 + qi[None, :]).astype(np.float64)
            qrow[g, 0:3, :, r, :] = np.stack(_split3(-sl * qpos))
            qrow[g, 3:6, :, r, :] = np.stack(_split3(np.full((16, 128), sl)))
            qrow[g, 6:9, :, r, :] = np.stack(_split3(np.full((16, 128), 128.0 * sl)))
    c["qrow"] = qrow.reshape(4, 9, 8192).astype(np.float32)
    kpos = np.concatenate([np.arange(T), 16 * np.arange(127) + 31]).astype(np.float64)
    krow = np.zeros((9, T + 127), np.float64)
    krow[0:3] = 1.0
    krow[3:6] = kpos % 128
    krow[6:9] = kpos // 128
    c["krow"] = krow.astype(np.float32)
    c["e2"] = (np.arange(T)[None, :] // 64 == np.arange(32)[:, None]).astype(np.float32)
    n = np.arange(127)
    q = np.arange(T)
    c["maskc"] = np.where(q[None, :] - (16 * n[:, None] + 31) >= 0, 0.0, -30000.0).astype(np.float32)
    blk = q // 64
    j = np.arange(32)
    bb = np.where(j[None, :] > blk[:, None], -1e30, 0.0)
    bb = np.where((j[None, :] == blk[:, None]) | (j[None, :] == 0), 1e9, bb)
    c["bbias"] = np.ascontiguousarray(bb.reshape(16, 128, 32).transpose(1, 0, 2).reshape(128, 512)).astype(np.float32)
    ki = np.arange(128)
    causal = np.where(qi[None, :] >= ki[:, None], 0.0, -30000.0)
    anti = np.where(qi[None, :] < ki[:, None], 0.0, -30000.0)
    c["cmask"] = np.concatenate([causal, anti], axis=1).astype(np.float32)
    c0 = np.arange(127)[:, None] * 16
    j0 = np.arange(32)[None, :] * 64
    c["ov"] = (np.clip(np.minimum(c0 + 32, j0 + 64) - np.maximum(c0, j0), 0, None) / 32).astype(np.float32)
    return c


def kernel(**inputs):
    global _NC
    f = lambda a: np.ascontiguousarray(np.asarray(a, dtype=np.float32))
    x = f(inputs["x"])
    common = {
        "gmixB": f(np.broadcast_to(inputs["norm_mix"][0][None, :], (128, D))),
        "gmlpB": f(np.broadcast_to(inputs["norm_mlp"][0][None, :], (128, D))),
        "w_in": f(inputs["w_in"][0]),
        "w_dn": f(inputs["w_proj_dn"][0]),
        "w_nsa": f(inputs["w_proj_nsa"][0]),
        "w_out": f(inputs["w_out"][0]),
        "w_up": f(inputs["w_up"][0]),
        "w_down": f(inputs["w_down"][0]),
        "ident": np.eye(128, dtype=np.float32),
    }
    ii = np.arange(128)
    ones = np.ones((128, 128), np.float32)
    ltri = (ii[:, None] <= ii[None, :]).astype(np.float32)
    maskA = np.where(ii[:, None] > ii[None, :], 0.0, 1e4).astype(np.float32)
    maskQ = np.where(ii[None, :] >= ii[:, None], 0.0, -1e4).astype(np.float32)
    common["cst"] = f(np.concatenate([ones, ltri, maskA, maskQ, np.eye(128, dtype=np.float32)], axis=1))
    cw = np.asarray(inputs["dn_conv"][0], np.float32)
    common["convw"] = f(cw.reshape(4, 24, 128).transpose(2, 1, 0).reshape(128, 96))
    sm = np.zeros((128, 384), np.float32)
    sm[:, 0:128] = np.tile(np.asarray(inputs["dn_a_log"][0], np.float32), 16)[None, :]
    sm[:, 128:256] = np.tile(np.asarray(inputs["dn_dt_bias"][0], np.float32), 16)[None, :]
    sm[:, 256] = np.asarray(inputs["dn_out_norm"][0], np.float32)
    common["gdnsm"] = sm
    common.update(_nsa_consts())
    nsm = np.zeros((128, 8), np.float32)
    for c, name in enumerate(["nsa_q_norm", "nsa_k_norm_cmp", "nsa_k_norm_slc", "nsa_k_norm_win"]):
        nsm[0:64, c] = np.asarray(inputs[name][0], np.float32)
    common["nsm"] = nsm
    common["w1k"] = f(inputs["cmp_w1_k"][0])
    common["w1v"] = f(inputs["cmp_w1_v"][0])
    common["w2k"] = f(inputs["cmp_w2_k"][0])
    common["w2v"] = f(inputs["cmp_w2_v"][0])
    common["posk"] = f(np.asarray(inputs["cmp_pos_k"][0]).T)
    common["posv"] = f(np.asarray(inputs["cmp_pos_v"][0]).T)
    if _NC is None:
        _NC = build_nc(DBG)
    in_maps = [dict(common, x=f(x[i])) for i in range(8)]
    res = run_bass_kernel_spmd(_NC, in_maps, core_ids=list(range(8)))
    if DBG:
        return res.results[0]["dbg"]
    return np.stack([r["out"] for r in res.results], axis=0).astype(np.float32)
```
